# Optimizing a Trainium2 kernel written in Bass

```python
import jax, jax.numpy as jnp
from jax import lax
import numpy as np

D_MODEL = 1024
BATCH = 4
SEQ = 8192
DEPTH = 1

CTX_LEN = 256
GRID_W = 64
M_HEADS = 4
M_DK = 128
M_DV = 128
M_QK = M_HEADS * M_DK
M_WIDTH = M_HEADS * M_DV
M_CHUNK = 128
CONV_K = 3
G_HEADS = 4
G_DK = 64
G_DV = 128
G_QK = G_HEADS * G_DK
G_WIDTH = G_HEADS * G_DV
G_RANK = 16
G_TAU = 16.0
G_CHUNK = 64
MIX_WIDTH = M_WIDTH + G_WIDTH
IN_COLS = 2 * M_QK + 2 * M_WIDTH + 4 * M_HEADS + 2 * G_QK + 2 * G_WIDTH + 2 * G_RANK
D_FF = ((8 * D_MODEL // 3 + 255) // 256) * 256
EPS = 1e-6
NEG = -1e30

kernel_name = "hymba_mlstm_gla_prefix_block"


def rmsnorm(x, g):
    xf = x.astype(jnp.float32)
    y = xf * lax.rsqrt(jnp.mean(xf * xf, -1, keepdims=True) + EPS)
    return (y * g.astype(jnp.float32)).astype(x.dtype)


def head_rmsnorm(h, g):
    b, nh, t, d = h.shape
    y = h * lax.rsqrt(jnp.mean(h * h, -1, keepdims=True) + EPS)
    y = jnp.transpose(y, (0, 2, 1, 3)).reshape(b, t, nh * d)
    return y * g.astype(jnp.float32)


def modulate(h, shift, scale):
    return h * (1 + scale) + shift


def heads(a, nh):
    b, t, w = a.shape
    return a.reshape(b, t, nh, w // nh).transpose(0, 2, 1, 3).astype(jnp.float32)


def flip(a):
    return jnp.flip(a, 2)


def dwconv2d(u, w, rows, cols):
    b, n, ch = u.shape
    img = u.reshape(b, rows, cols, ch)
    out = lax.conv_general_dilated(img, w[:, :, None, :].astype(u.dtype), window_strides=(1, 1), padding='SAME',
                                   dimension_numbers=('NHWC', 'HWIO', 'NHWC'), feature_group_count=ch)
    return out.reshape(b, n, ch)


def to_chunks(a, chunk):
    b, h, t = a.shape[:3]
    a = a.reshape(b, h, t // chunk, chunk, *a.shape[3:])
    return jnp.moveaxis(a, 2, 0)


def from_chunks(a):
    a = jnp.moveaxis(a, 0, 2)
    return a.reshape(a.shape[0], a.shape[1], -1, a.shape[-1])


def mlstm_scan(q, k, v, logi, logf, state):
    xs = tuple(to_chunks(a, M_CHUNK) for a in (q, k, v, logi, logf))
    lower = jnp.tril(jnp.ones((M_CHUNK, M_CHUNK), bool))

    def step(carry, inp):
        C, nv, m = carry
        qc, kc, vc, ic, fc = inp
        bcum = jnp.cumsum(fc, -1)
        dmat = jnp.where(lower, bcum[..., :, None] - bcum[..., None, :] + ic[..., None, :], NEG)
        inter = m[..., None] + bcum
        m_t = jnp.maximum(inter, jnp.max(dmat, -1))
        w_inter = jnp.exp(inter - m_t)
        scores = jnp.einsum('bhtd,bhsd->bhts', qc, kc) * jnp.exp(dmat - m_t[..., None])
        num = w_inter[..., None] * jnp.einsum('bhtd,bhde->bhte', qc, C) + jnp.einsum('bhts,bhse->bhte', scores, vc)
        den = w_inter * jnp.einsum('bhtd,bhd->bht', qc, nv) + jnp.sum(scores, -1)
        hout = num / jnp.maximum(jnp.abs(den), jnp.exp(-m_t))[..., None]
        btot = bcum[..., -1]
        src = btot[..., None] - bcum + ic
        m_new = jnp.maximum(m + btot, jnp.max(src, -1))
        ws = jnp.exp(src - m_new[..., None])
        wc = jnp.exp(m + btot - m_new)
        C_new = wc[..., None, None] * C + jnp.einsum('bhs,bhsd,bhse->bhde', ws, kc, vc)
        n_new = wc[..., None] * nv + jnp.einsum('bhs,bhsd->bhd', ws, kc)
        return (C_new, n_new, m_new), hout

    state, hs = lax.scan(step, state, xs)
    return from_chunks(hs), state


def mlstm_state(k, v, logi, logf):
    bcum = jnp.cumsum(logf, -1)
    src = bcum[..., -1:] - bcum + logi
    m = jnp.max(src, -1)
    w = jnp.exp(src - m[..., None])
    return (jnp.einsum('bhs,bhsd,bhse->bhde', w, k, v), jnp.einsum('bhs,bhsd->bhd', w, k), m)


def gla_scan(q, k, v, loga, S):
    xs = tuple(to_chunks(a, G_CHUNK) for a in (q, k, v, loga))
    lower = jnp.tril(jnp.ones((G_CHUNK, G_CHUNK), bool))[..., None]

    def step(S, inp):
        qc, kc, vc, ac = inp
        bcum = jnp.cumsum(ac, -2)
        decay = jnp.where(lower, bcum[..., :, None, :] - bcum[..., None, :, :], NEG)
        att = jnp.einsum('bhtk,bhsk,bhtsk->bhts', qc, kc, jnp.exp(decay))
        out = jnp.einsum('bhtk,bhkv->bhtv', qc * jnp.exp(bcum), S) + jnp.einsum('bhts,bhsv->bhtv', att, vc)
        btot = bcum[..., -1:, :]
        S_new = jnp.exp(btot[..., 0, :])[..., None] * S + jnp.einsum('bhsk,bhsv->bhkv', kc * jnp.exp(btot - bcum), vc)
        return S_new, out

    S, outs = lax.scan(step, S, xs)
    return from_chunks(outs), S


def gla_state(k, v, loga):
    bcum = jnp.cumsum(loga, -2)
    return jnp.einsum('bhsk,bhsv->bhkv', k * jnp.exp(bcum[..., -1:, :] - bcum), v)


def mixer_features(h, w_in, conv_w, m_gate_b, g_gate_w, g_gate_b, rows, cols):
    z = h @ w_in
    sizes = (2 * M_QK, M_WIDTH, M_WIDTH, 4 * M_HEADS, G_QK, G_QK, G_WIDTH, G_WIDTH, 2 * G_RANK)
    parts, off = [], 0
    for s in sizes:
        parts.append(z[..., off:off + s])
        off += s
    mqk, mv, mo, mg, gq, gk, gv, gr, glr = parts
    mqk = jax.nn.silu(dwconv2d(mqk, conv_w, rows, cols))
    mq, mk = jnp.split(mqk, 2, -1)
    b, t, _ = h.shape
    mg = (mg + m_gate_b).astype(jnp.float32).reshape(b, t, 4, M_HEADS).transpose(2, 0, 3, 1)
    loga = [jax.nn.log_sigmoid((glr[..., i * G_RANK:(i + 1) * G_RANK] @ g_gate_w[i] + g_gate_b[i]).astype(jnp.float32)) / G_TAU
            for i in range(2)]
    return dict(
        mq=heads(mq, M_HEADS), mk=heads(mk, M_HEADS) * (M_DK ** -0.5), mv=heads(mv, M_HEADS), mo=mo,
        mi_f=mg[0], mf_f=jax.nn.log_sigmoid(mg[1]), mi_b=mg[2], mf_b=jax.nn.log_sigmoid(mg[3]),
        gq=heads(gq, G_HEADS) * (G_DK ** -0.5), gk=heads(gk, G_HEADS), gv=heads(gv, G_HEADS), gr=gr,
        ga_f=heads(loga[0], G_HEADS), ga_b=heads(loga[1], G_HEADS))


def zero_states(b):
    f32 = jnp.float32
    ms = (jnp.zeros((b, M_HEADS, M_DK, M_DV), f32), jnp.zeros((b, M_HEADS, M_DK), f32), jnp.full((b, M_HEADS), NEG, f32))
    gs = jnp.zeros((b, G_HEADS, G_DK, G_DV), f32)
    return (ms, ms, gs, gs)


def context_states(f):
    return (mlstm_state(f['mk'], f['mv'], f['mi_f'], f['mf_f']),
            mlstm_state(flip(f['mk']), flip(f['mv']), flip(f['mi_b']), flip(f['mf_b'])),
            gla_state(f['gk'], f['gv'], f['ga_f']),
            gla_state(flip(f['gk']), flip(f['gv']), flip(f['ga_b'])))


def mixer_apply(f, states, m_norm_g, g_norm_g, w_out):
    smf, smb, sgf, sgb = states
    hf, smf = mlstm_scan(f['mq'], f['mk'], f['mv'], f['mi_f'], f['mf_f'], smf)
    hb, smb = mlstm_scan(flip(f['mq']), flip(f['mk']), flip(f['mv']), flip(f['mi_b']), flip(f['mf_b']), smb)
    hm = head_rmsnorm(hf + flip(hb), m_norm_g) * jax.nn.sigmoid(f['mo'].astype(jnp.float32))
    of, sgf = gla_scan(f['gq'], f['gk'], f['gv'], f['ga_f'], sgf)
    ob, sgb = gla_scan(flip(f['gq']), flip(f['gk']), flip(f['gv']), flip(f['ga_b']), sgb)
    hg = head_rmsnorm(of + flip(ob), g_norm_g) * jax.nn.silu(f['gr'].astype(jnp.float32))
    out = jnp.concatenate([hm, hg], -1).astype(w_out.dtype) @ w_out
    return out, (smf, smb, sgf, sgb)


def swiglu(h, w_gu, w_down):
    a, g = jnp.split(h @ w_gu, 2, -1)
    return (jax.nn.silu(a) * g) @ w_down


def setup_inputs(seed: int = 0) -> dict:
    key = jax.random.key(seed)
    ks = jax.random.split(key, 24)
    f32 = jnp.float32
    D = D_MODEL

    def nrm(k, shape, s):
        return s * jax.random.normal(k, shape, f32)

    fb = jnp.linspace(3.0, 6.0, M_HEADS, dtype=f32)
    zb = jnp.zeros((M_HEADS,), f32)
    m_gate_base = jnp.concatenate([zb, fb, zb, fb])
    return {
        "x": nrm(ks[0], (BATCH, SEQ, D), 1.0),
        "c": nrm(ks[1], (BATCH, D), 1.0),
        "ctx": nrm(ks[2], (BATCH, CTX_LEN, D), 1.0),
        "c_ctx": nrm(ks[3], (D,), 1.0),
        "w_ada": nrm(ks[4], (DEPTH, D, 6 * D), 0.5 * D ** -0.5),
        "b_ada": nrm(ks[5], (DEPTH, 6 * D), 0.01),
        "g_mix": 1.0 + nrm(ks[6], (DEPTH, D), 0.02),
        "w_in": nrm(ks[7], (DEPTH, D, IN_COLS), D ** -0.5),
        "conv_w": nrm(ks[8], (DEPTH, CONV_K, CONV_K, 2 * M_QK), 1.0 / CONV_K),
        "m_gate_b": m_gate_base + nrm(ks[9], (DEPTH, 4 * M_HEADS), 0.1),
        "m_norm_g": 1.0 + nrm(ks[10], (DEPTH, M_WIDTH), 0.02),
        "g_gate_w": nrm(ks[11], (DEPTH, 2, G_RANK, G_QK), G_RANK ** -0.5),
        "g_gate_b": nrm(ks[12], (DEPTH, 2, G_QK), 0.01),
        "g_norm_g": 1.0 + nrm(ks[13], (DEPTH, G_WIDTH), 0.02),
        "w_out": nrm(ks[14], (DEPTH, MIX_WIDTH, D), MIX_WIDTH ** -0.5),
        "g_ffn": 1.0 + nrm(ks[15], (DEPTH, D), 0.02),
        "w_gu": nrm(ks[16], (DEPTH, D, 2 * D_FF), D ** -0.5),
        "w_down": nrm(ks[17], (DEPTH, D_FF, D), D_FF ** -0.5),
        "g_final": 1.0 + nrm(ks[18], (D,), 0.02),
    }


def reference(x, c, ctx, c_ctx, w_ada, b_ada, g_mix, w_in, conv_w, m_gate_b, m_norm_g, g_gate_w, g_gate_b,
              g_norm_g, w_out, g_ffn, w_gu, w_down, g_final):
    rows = x.shape[1] // GRID_W
    h_ctx = ctx
    for l in range(DEPTH):
        last = l == DEPTH - 1
        mod_x = (jax.nn.silu(c) @ w_ada[l] + b_ada[l])[:, None, :]
        mod_c = jax.nn.silu(c_ctx) @ w_ada[l] + b_ada[l]
        shx1, scx1, gx1, shx2, scx2, gx2 = jnp.split(mod_x, 6, -1)
        shc1, scc1, gc1, shc2, scc2, gc2 = jnp.split(mod_c, 6, -1)
        proj = (w_in[l], conv_w[l], m_gate_b[l], g_gate_w[l], g_gate_b[l])
        fc = mixer_features(modulate(rmsnorm(h_ctx, g_mix[l]), shc1, scc1), *proj, 1, h_ctx.shape[1])
        fx = mixer_features(modulate(rmsnorm(x, g_mix[l]), shx1, scx1), *proj, rows, GRID_W)
        if last:
            ctx_st = context_states(fc)
        else:
            oc, ctx_st = mixer_apply(fc, zero_states(h_ctx.shape[0]), m_norm_g[l], g_norm_g[l], w_out[l])
            h_ctx = h_ctx + gc1 * oc
            h_ctx = h_ctx + gc2 * swiglu(modulate(rmsnorm(h_ctx, g_ffn[l]), shc2, scc2), w_gu[l], w_down[l])
        ox, _ = mixer_apply(fx, ctx_st, m_norm_g[l], g_norm_g[l], w_out[l])
        x = x + gx1 * ox
        x = x + gx2 * swiglu(modulate(rmsnorm(x, g_ffn[l]), shx2, scx2), w_gu[l], w_down[l])
    return rmsnorm(x, g_final)
```

```python
import contextlib
import math
import os
import numpy as np
import concourse.bass as bass
import concourse.mybir as mybir
from concourse.bass_utils import run_bass_kernel_spmd

F32 = mybir.dt.float32
BF16 = mybir.dt.bfloat16
AF = mybir.ActivationFunctionType
ALU = mybir.AluOpType
AX = mybir.AxisListType

D = 1024
SEQ = 8192
OWN = 4096
CTX = 256
NTOK = SEQ + CTX
NCH = NTOK // 128
DFF = 2816
EPS = 1e-6
FM_COLS = 1568
TM_COLS = 2064
LN_MSCALE = math.log(128.0 ** -0.5)

DEBUG = bool(int(os.environ.get("KDEBUG", "0")))
STOP_AFTER = int(os.environ.get("KSTOP", "9"))


class Buf:
    __slots__ = ("name", "w", "r", "dsem", "psum")

    def __init__(self, name):
        self.name = name
        self.psum = False
        self.w = {}
        self.r = {}
        self.dsem = None


class Tl:
    def __init__(self, t, name):
        self.t = t
        self.b = Buf(name)

    def __getitem__(self, k):
        return self.t[k]


class Sched:
    def __init__(self, nc, es):
        self.nc = nc
        self.es = es
        self.eng = {}
        for n, h in (("pe", nc.tensor), ("dve", nc.vector), ("act", nc.scalar),
                     ("pool", nc.gpsimd), ("sp", nc.sync)):
            sem = es.enter_context(nc.semaphore("s_" + n))
            self.eng[n] = dict(h=h, sem=sem, cnt=0, waited={})
        self.dsems = []
        self.nops = 0
        self.free = []
        self.phase_sems = None

    def phase_begin(self):
        self.phase_sems = []

    def phase_end(self):
        self.barrier()
        if self.phase_sems:
            self.free += self.phase_sems
        self.phase_sems = None

    def _wait(self, en, evs):
        e = self.eng[en]
        need = {}
        for (sem, val) in evs:
            k = id(sem)
            if k not in need or need[k][1] < val:
                need[k] = (sem, val)
        for k, (sem, val) in need.items():
            if e["waited"].get(k, 0) < val:
                e["h"].wait_ge(sem, val)
                e["waited"][k] = val

    def _deps(self, en, R, W, skip_own):
        own = id(self.eng[en]["sem"])
        evs = []
        for b in R:
            b = b.b if isinstance(b, Tl) else b
            for k, ev in b.w.items():
                if skip_own and k == own:
                    continue
                evs.append(ev)
            if b.psum:
                for k, ev in b.r.items():
                    if k != own:
                        evs.append(ev)
        for b in W:
            b = b.b if isinstance(b, Tl) else b
            for k, ev in b.w.items():
                if k != own:
                    evs.append(ev)
            for k, ev in b.r.items():
                if k != own:
                    evs.append(ev)
        return evs

    def _record(self, ev, R, W):
        k = id(ev[0])
        for b in R:
            b = b.b if isinstance(b, Tl) else b
            b.r[k] = ev
        for b in W:
            b = b.b if isinstance(b, Tl) else b
            b.w[k] = ev

    def op(self, en, fn, R=(), W=()):
        e = self.eng[en]
        self._wait(en, self._deps(en, R, W, skip_own=(en == "pe")))
        ins = fn(e["h"])
        ins.then_inc(e["sem"], 1)
        e["cnt"] += 1
        self._record((e["sem"], e["cnt"]), R, W)
        self.nops += 1

    def dma(self, out, in_, R=(), W=(), owner=None, q="sp"):
        owner = owner if owner is not None else W[0]
        ob = owner.b if isinstance(owner, Tl) else owner
        if ob.dsem is None:
            if self.free:
                ob.dsem = self.free.pop()
            else:
                ob.dsem = [self.es.enter_context(self.nc.semaphore("d_%d" % len(self.dsems))), 0]
                self.dsems.append(ob.dsem)
            if self.phase_sems is not None:
                self.phase_sems.append(ob.dsem)
        e = self.eng[q]
        self._wait(q, self._deps(q, R, W, skip_own=False))
        ins = e["h"].dma_start(out=out, in_=in_)
        ob.dsem[1] += 16
        ins.then_inc(ob.dsem[0], 16)
        self._record((ob.dsem[0], ob.dsem[1]), R, W)
        self.nops += 1

    def all_events(self):
        evs = [(e["sem"], e["cnt"]) for e in self.eng.values() if e["cnt"] > 0]
        evs += [(d[0], d[1]) for d in self.dsems if d[1] > 0]
        return evs

    def barrier(self):
        evs = self.all_events()
        for en in self.eng:
            self._wait(en, evs)


def build_program():
    nc = bass.Bass("TRN2", target_bir_lowering=False)

    def din(name, shape, dt=F32):
        return nc.dram_tensor(name, list(shape), dt, kind="ExternalInput").ap()

    def dscr(name, shape, dt):
        kind = "ExternalOutput" if DEBUG else "Internal"
        return Tl(nc.dram_tensor(name, list(shape), dt, kind=kind).ap(), name)

    x_in = din("x_l", [SEQ, D])
    ctx_in = din("ctx_l", [CTX, D])
    cvec_in = din("cvec", [128, 8, 2])
    wada_in = din("w_ada", [D, 6 * D])
    bada_in = din("b_ada_c", [128, 48])
    gmix_in = din("gmix_c", [128, 8])
    gffn_in = din("gffn_c", [128, 8])
    gfin_in = din("gfin_r", [1, D])
    wfm_in = din("w_fm", [D, FM_COLS])
    wtm_in = din("w_tm", [D, TM_COLS])
    convw_in = din("convw_c", [128, 9, 8])
    mgb_in = din("mgb_r", [1, 16])
    ggw_in = din("ggw_e", [2, 17, 256])
    gn_in = din("gn_c", [128, 8])
    wout_in = din("w_out", [D, D])
    wgu_in = din("w_gu", [D, 2 * DFF])
    wdn_in = din("w_down", [DFF, D])
    y_out = nc.dram_tensor("y", [OWN, D], F32, kind="ExternalOutput").ap()
    y_buf = Buf("y")

    qT_d = dscr("qT_d", [512, OWN], BF16)
    kT_d = dscr("kT_d", [512, OWN], BF16)
    ktok_d = dscr("ktok_d", [NTOK, 512], BF16)
    v_d = dscr("v_d", [NTOK, 512], BF16)
    mo_d = dscr("mo_d", [OWN, 512], BF16)
    G_d = dscr("G_d", [128, NCH, 16], F32)
    gqT_d = dscr("gqT_d", [256, OWN], BF16)
    gkT_d = dscr("gkT_d", [256, NTOK], BF16)
    gv_d = dscr("gv_d", [NTOK, 512], BF16)
    gr_d = dscr("gr_d", [OWN, 512], BF16)
    lgf_d = dscr("lgf_d", [NTOK, 256], F32)
    lgb_d = dscr("lgb_d", [NTOK, 256], F32)
    hb_d = dscr("hb_d", [OWN, D], F32)
    x1_d = dscr("x1_d", [OWN, D], F32)

    dbgC, dbgS, dbgCb = {}, {}, Buf("dbg")
    if DEBUG:
        for nm in ("cb", "cf", "ob"):
            dbgC[nm] = nc.dram_tensor("dbgC_" + nm, [128, 516], F32, kind="ExternalOutput").ap()
            dbgS[nm] = nc.dram_tensor("dbgS_" + nm, [128, 256], F32, kind="ExternalOutput").ap()

    with contextlib.ExitStack() as ges:
        S = Sched(nc, ges)
        uid = [0]

        def sb(es, shape, dt, name=None):
            uid[0] += 1
            nm = (name or "t") + "_%d" % uid[0]
            return Tl(es.enter_context(nc.sbuf_tensor(nm, list(shape), dt)), nm)

        def ps(es, shape, dt, name=None):
            uid[0] += 1
            nm = (name or "p") + "_%d" % uid[0]
            t = Tl(es.enter_context(nc.psum_tensor(nm, list(shape), dt)), nm)
            t.b.psum = True
            return t

        identb = sb(ges, [128, 128], BF16, "identb")
        identf = sb(ges, [128, 128], F32, "identf")
        onesf = sb(ges, [128, 128], F32, "onesf")
        maskf = [sb(ges, [128, 128], F32, "mask%d" % d) for d in range(2)]
        modc = sb(ges, [128, 48, 2], F32, "modc")
        A1 = sb(ges, [128, 8, 2], F32, "A1")
        A2 = sb(ges, [128, 8], F32, "A2")
        gmix = sb(ges, [128, 8], F32, "gmix")
        gffn = sb(ges, [128, 8], F32, "gffn")
        gncol = sb(ges, [128, 8], F32, "gncol")
        lnms = sb(ges, [128, 1], F32, "lnms")
        epsg = sb(ges, [128, 1], F32, "epsg")
        S.op("pool", lambda h: h.memset(epsg[:], EPS), W=[epsg])
        S.op("pool", lambda h: h.memset(lnms[:], LN_MSCALE), W=[lnms])

        S.op("pool", lambda h: h.memset(onesf[:], 1.0), W=[onesf])
        S.op("pool", lambda h: h.memset(identf[:], 1.0), W=[identf])
        S.op("pool", lambda h: h.affine_select(out=identf[:], in_=identf[:], pattern=[[-1, 128]],
                                               compare_op=ALU.is_equal, fill=0.0, base=0,
                                               channel_multiplier=1), R=[identf], W=[identf])
        S.op("dve", lambda h: h.tensor_copy(out=identb[:], in_=identf[:]), R=[identf], W=[identb])
        for d in range(2):
            S.op("pool", lambda h: h.memset(maskf[d][:], 1.0), W=[maskf[d]])
            sg = 1 if d == 0 else -1
            S.op("pool", lambda h: h.affine_select(out=maskf[d][:], in_=maskf[d][:], pattern=[[sg, 128]],
                                                   compare_op=ALU.is_ge, fill=0.0, base=0,
                                                   channel_multiplier=-sg), R=[maskf[d]], W=[maskf[d]])
        S.dma(gmix[:], gmix_in[:, :], W=[gmix])
        S.dma(gffn[:], gffn_in[:, :], W=[gffn])
        S.dma(gncol[:], gn_in[:, :], W=[gncol])

        S.phase_begin()
        with contextlib.ExitStack() as es:
            cv = sb(es, [128, 8, 2], F32, "cv")
            sc = sb(es, [128, 8, 2], F32, "sc")
            bad = sb(es, [128, 48], F32, "bad")
            wst = [sb(es, [128, 6 * D], F32, "wst") for _ in range(4)]
            pm = ps(es, [128, 48, 2], F32, "pm")
            S.dma(cv[:], cvec_in[:, :, :], W=[cv])
            S.dma(bad[:], bada_in[:, :], W=[bad])
            S.op("act", lambda h: h.activation(out=sc[:], in_=cv[:], func=AF.Silu), R=[cv], W=[sc])
            S.op("dve", lambda h: h.memset(pm[:], 0.0), W=[pm])
            for k in range(8):
                w = wst[k % 4]
                for q4 in range(4):
                    S.dma(w[:, q4 * 1536:(q4 + 1) * 1536], wada_in[k * 128:(k + 1) * 128, q4 * 1536:(q4 + 1) * 1536], W=[w])
                for j in range(48):
                    S.op("pe", lambda h: h.matmul(pm[:, j, :], lhsT=w[:, j * 128:(j + 1) * 128], rhs=sc[:, k, :],
                                                  start=False, stop=(k == 7), skip_group_check=True), R=[w, sc], W=[pm])
            for r in range(2):
                S.op("dve", lambda h: h.tensor_tensor(out=modc[:, :, r], in0=pm[:, :, r], in1=bad[:],
                                                      op=ALU.add), R=[pm, bad], W=[modc])
            for r in range(2):
                S.op("dve", lambda h: h.scalar_tensor_tensor(out=A1[:, :, r], in0=modc[:, 8:16, r], scalar=1.0,
                                                             in1=gmix[:], op0=ALU.add, op1=ALU.mult),
                     R=[modc, gmix], W=[A1])
            S.op("dve", lambda h: h.scalar_tensor_tensor(out=A2[:], in0=modc[:, 32:40, 0], scalar=1.0,
                                                         in1=gffn[:], op0=ALU.add, op1=ALU.mult),
                 R=[modc, gffn], W=[A2])
        S.phase_end()

        def B1(r, k):
            return modc[:, k, r:r + 1]

        def rstd_from_ss(ss, rs, n, scale):
            S.op("act", lambda h: h.activation(out=rs[:, 0:n], in_=ss[:, 0:n], func=AF.Ln, scale=scale, bias=epsg[:, 0:1]),
                 R=[ss, epsg], W=[rs])
            S.op("act", lambda h: h.activation(out=rs[:, 0:n], in_=rs[:, 0:n], func=AF.Exp, scale=-0.5), R=[rs], W=[rs])

        if STOP_AFTER >= 1 and not os.environ.get('KSKIP1'):
          S.phase_begin()
          with contextlib.ExitStack() as es:
            wfm = sb(es, [128, 8, FM_COLS], BF16, "wfm")
            wtm = sb(es, [128, 8, TM_COLS], BF16, "wtm")
            dgw = sb(es, [128, 72, 128], BF16, "dgw")
            cw = sb(es, [128, 9, 8], F32, "cw")
            ggwf = sb(es, [17, 2, 256], F32, "ggwf")
            ggw = sb(es, [17, 2, 256], BF16, "ggw")
            mgb = sb(es, [128, 16], F32, "mgb")
            S.dma(cw[:], convw_in[:, :, :], W=[cw])
            S.dma(mgb[:], mgb_in.broadcast_to([128, 16]), W=[mgb])
            S.dma(ggwf[:], ggw_in.rearrange("d k n -> k d n"), W=[ggwf])
            S.op("dve", lambda h: h.tensor_copy(out=ggw[:], in_=ggwf[:]), R=[ggwf], W=[ggw])
            with contextlib.ExitStack() as ses1:
                stg = [sb(ses1, [128, TM_COLS], F32, "stg") for _ in range(5)]
                ci1 = 0
                engs = ("dve", "pool", "act")
                for k in range(8):
                    for (dst, src, ncol) in ((wfm, wfm_in, FM_COLS), (wtm, wtm_in, TM_COLS)):
                        st = stg[ci1 % 5]
                        en = engs[ci1 % 3]
                        ci1 += 1
                        S.dma(st[:, 0:ncol], src[k * 128:(k + 1) * 128, :], W=[st])
                        if en == "act":
                            S.op("act", lambda h: h.activation(out=dst[:, k, :], in_=st[:, 0:ncol], func=AF.Copy), R=[st], W=[dst])
                        else:
                            S.op(en, lambda h: h.tensor_copy(out=dst[:, k, :], in_=st[:, 0:ncol]), R=[st], W=[dst])
            S.barrier()
            for tap in range(9):
                for ch in range(8):
                    S.op("dve", lambda h: h.tensor_scalar(out=dgw[:, tap * 8 + ch, :], in0=identf[:],
                                                          scalar1=cw[:, tap, ch:ch + 1], scalar2=None,
                                                          op0=ALU.mult), R=[identf, cw], W=[dgw])

            xt = [sb(es, [128, D], F32, "xt") for _ in range(6)]
            xn = [[sb(es, [128, D], BF16, "xn") for _ in range(5)] for _ in range(2)]
            ssq = [sb(es, [128, 8], F32, "ssq") for _ in range(2)]
            rsq = [sb(es, [128, 8], F32, "rsq") for _ in range(2)]
            hT = [sb(es, [128, 8, 640], BF16, "hT") for _ in range(2)]
            zT = sb(es, [128, 8, 640], BF16, "zT")
            qk = [sb(es, [128, 512], BF16, "qk") for _ in range(3)]
            kTs = sb(es, [128, 4, 512], BF16, "kTs")
            kto = [sb(es, [128, 512], BF16, "kto") for _ in range(2)]
            gxo = [sb(es, [128, 512], BF16, "gxo") for _ in range(2)]
            glrT = [sb(es, [17, 512], BF16, "glrT") for _ in range(2)]
            lgt = sb(es, [128, 512], F32, "lgt")
            lgo = [sb(es, [128, 512], F32, "lgo") for _ in range(2)]
            tmo = [[sb(es, [128, 512], BF16, "tmo") for _ in range(2)] for _ in range(4)]
            Gt = [sb(es, [128, 16], F32, "Gt") for _ in range(2)]
            Gtmp = sb(es, [128, 8], F32, "Gtmp")
            pTl = [ps(es, [128, 512], F32, "pT") for _ in range(2)]

            def bfv(t):
                return t[:].bitcast(BF16).rearrange("p (a b) -> p a b", b=128)
            pF = [ps(es, [128, 512], F32, "pF") for _ in range(2)]
            pCVl = [ps(es, [128, 512], F32, "pCV") for _ in range(2)]
            pTM = [ps(es, [128, 512], F32, "pTM") for _ in range(2)]
            for d in range(2):
                S.op("pool", lambda h: h.memset(glrT[d][:], 1.0), W=[glrT[d]])

            cnt = dict(x=0, hT=0, qk=0, kto=0, gxo=0, lgo=0, tm=0, G=0, ev=0, pf=0, ptm=0, pt=0, pcv=0, ab=0)
            allb = [pF[0], pCVl[0], pTM[0], pTl[0], pF[1], pCVl[1], pTM[1], pTl[1]]

            def evac_copy(out, in_, R, W, scale=None):
                if scale is None:
                    S.op("dve", lambda h: h.tensor_copy(out=out, in_=in_), R=R, W=W)
                else:
                    S.op("dve", lambda h: h.tensor_scalar(out=out, in0=in_, scalar1=scale, scalar2=None,
                                                          op0=ALU.mult), R=R, W=W)

            blocks = []
            for j in range(16):
                blocks.append(dict(src="x", t0=512 * j, nt=512, halo=64, full=(j < 8), need_f=(j < 8), r=0))
            blocks.append(dict(src="ctx", t0=SEQ, nt=CTX, halo=0, full=False, need_f=True, r=1))

            def stage1a(bi):
                blk = blocks[bi]
                t0, nt, halo = blk["t0"], blk["nt"], blk["halo"]
                isx = blk["src"] == "x"
                ntile = (nt + 2 * halo) // 128
                ssl, rsl = ssq[bi % 2], rsq[bi % 2]
                xts = []
                for i in range(ntile):
                    xtl = xt[cnt["x"] % len(xt)]
                    cnt["x"] += 1
                    xts.append(xtl)
                    xnl = xn[bi % 2][i]
                    if isx:
                        lo = t0 - halo + 128 * i
                        hi = lo + 128
                        clo, chi = max(lo, 0), min(hi, SEQ)
                        if clo > lo or chi < hi:
                            S.op("pool", lambda h: h.memset(xtl[:], 0.0), W=[xtl])
                        S.dma(xtl[clo - lo:chi - lo, :], x_in[clo:chi, :], W=[xtl])
                    else:
                        S.dma(xtl[:], ctx_in[128 * i:128 * (i + 1), :], W=[xtl])
                    S.op("act", lambda h: h.activation(out=xnl[:], in_=xtl[:], func=AF.Square,
                                                       accum_out=ssl[:, i:i + 1]), R=[xtl], W=[xnl, ssl])
                rstd_from_ss(ssl, rsl, ntile, 1.0 / D)
                for i in range(ntile):
                    xnl = xn[bi % 2][i]
                    S.op("act", lambda h: h.activation(out=xnl[:], in_=xts[i][:], func=AF.Copy,
                                                       scale=rsl[:, i:i + 1]), R=[xts[i], rsl], W=[xnl])

            def stage1b(bi):
                blk = blocks[bi]
                nt, halo, r = blk["nt"], blk["halo"], blk["r"]
                ntile = (nt + 2 * halo) // 128
                h_ = hT[bi % 2]
                def tile_item(i):
                    xnl = xn[bi % 2][i]
                    pT = allb[cnt["ab"] % 8]
                    cnt["ab"] += 1
                    for k in range(8):
                        S.op("pe", lambda h: h.transpose(out=bfv(pT)[:, k, :], in_=xnl[:, k * 128:(k + 1) * 128],
                                                         identity=identb[:]), R=[xnl, identb], W=[pT])
                    for k in range(8):
                        o = h_[:, k, i * 128:(i + 1) * 128]
                        if True:
                            S.op("dve", lambda h: h.tensor_scalar(out=o, in0=bfv(pT)[:, k, :], scalar1=A1[:, k, r:r + 1],
                                                                  scalar2=B1(r, k), op0=ALU.mult, op1=ALU.add),
                                 R=[pT, A1, modc], W=[h_])
                        else:
                            S.op("act", lambda h: h.activation(out=o, in_=bfv(pT)[:, k, :], func=AF.Identity,
                                                               scale=A1[:, k, r:r + 1], bias=B1(r, k)),
                                 R=[pT, A1, modc], W=[h_])
                return [(lambda i=i: tile_item(i)) for i in range(ntile)]

            def interleave(a, b):
                a, b = list(a), list(b)
                n = max(len(a), len(b))
                for i in range(n):
                    if i < len(a):
                        a[i]()
                    lo = (i * len(b)) // n if n else 0
                    hi = ((i + 1) * len(b)) // n if n else 0
                    for j in range(lo, hi):
                        b[j]()

            stage1a(0)
            for it in stage1b(0):
                it()
            for bi, blk in enumerate(blocks):
                t0, nt, halo, full, r = blk["t0"], blk["nt"], blk["halo"], blk["full"], blk["r"]
                isx = blk["src"] == "x"
                W_ = nt + 2 * halo
                ntile = W_ // 128
                h_ = hT[bi % 2]
                if bi + 1 < len(blocks):
                    stage1a(bi + 1)
                c0 = halo
                mch = list(range(8)) if full else list(range(4, 8))
                nhalf = 2 if isx else 1
                hw = W_ // nhalf
                for ch in mch:
                    for hf in range(nhalf):
                        p = allb[cnt["ab"] % 8]
                        cnt["ab"] += 1
                        for k in range(8):
                            S.op("pe", lambda h: h.matmul(p[:, 0:hw], lhsT=wfm[:, k, ch * 128:(ch + 1) * 128],
                                                          rhs=h_[:, k, hf * hw:(hf + 1) * hw],
                                                          start=(k == 0), stop=(k == 7)), R=[wfm, h_], W=[p])
                        evac_copy(zT[:, ch, hf * hw:(hf + 1) * hw], p[:, 0:hw], [p], [zT])
                if isx and t0 == 0:
                    S.op("pool", lambda h: h.memset(zT[:, :, 0:64], 0.0), W=[zT])
                if isx and t0 + nt == SEQ:
                    S.op("pool", lambda h: h.memset(zT[:, :, 576:640], 0.0), W=[zT])
                if isx:
                    rows, cols, taps = 8, 64, [(ky, kx) for ky in range(3) for kx in (1, 0, 2)]
                else:
                    rows, cols, taps = 1, 256, [(1, 1), (1, 0), (1, 2)]
                def conv_item(ch):
                    pCV = allb[cnt["ab"] % 8]
                    cnt["ab"] += 1
                    zv = zT[:, ch, 0:W_].rearrange("p (a b) -> p a b", b=cols)
                    cvv = pCV[:, 0:nt].rearrange("p (a b) -> p a b", b=cols)
                    for ti, (ky, kx) in enumerate(taps):
                        dy, dx = ky - 1, kx - 1
                        r0 = (1 + dy) if isx else 0
                        if dx == 0:
                            o_ap, i_ap = cvv[:, :, :], zv[:, r0:r0 + rows, :]
                        elif dx == -1:
                            o_ap, i_ap = cvv[:, :, 1:cols], zv[:, r0:r0 + rows, 0:cols - 1]
                        else:
                            o_ap, i_ap = cvv[:, :, 0:cols - 1], zv[:, r0:r0 + rows, 1:cols]
                        S.op("pe", lambda h: h.matmul(o_ap, lhsT=dgw[:, (ky * 3 + kx) * 8 + ch, :], rhs=i_ap,
                                                      start=(ti == 0), stop=(ti == len(taps) - 1)),
                             R=[dgw, zT], W=[pCV])
                    if ch < 4:
                        o = qk[cnt["qk"] % 3]
                        cnt["qk"] += 1
                        S.op("act", lambda h: h.activation(out=o[:, 0:nt], in_=pCV[:, 0:nt], func=AF.Silu),
                             R=[pCV], W=[o])
                        S.dma(qT_d[ch * 128:(ch + 1) * 128, t0:t0 + nt], o[:, 0:nt], R=[o], W=[qT_d], owner=o)
                    else:
                        S.op("act", lambda h: h.activation(out=kTs[:, ch - 4, 0:nt], in_=pCV[:, 0:nt], func=AF.Silu),
                             R=[pCV], W=[kTs])
                        if full:
                            S.dma(kT_d[(ch - 4) * 128:(ch - 3) * 128, t0:t0 + nt], kTs[:, ch - 4, 0:nt],
                                  R=[kTs], W=[kT_d], owner=kTs)
                conv_items = [(lambda ch=ch: conv_item(ch)) for ch in mch]
                interleave(conv_items, stage1b(bi + 1) if bi + 1 < len(blocks) else [])
                def ktr_item(ti):
                    pT2 = allb[cnt["ab"] % 8]
                    cnt["ab"] += 1
                    for hh in range(4):
                        S.op("pe", lambda h: h.transpose(out=bfv(pT2)[:, hh, :], in_=kTs[:, hh, ti * 128:(ti + 1) * 128],
                                                         identity=identb[:]), R=[kTs, identb], W=[pT2])
                    o = kto[cnt["kto"] % 2]
                    cnt["kto"] += 1
                    evac_copy(o[:].rearrange("p (a b) -> p a b", b=128), bfv(pT2)[:, 0:4, :], [pT2], [o])
                    S.dma(ktok_d[t0 + ti * 128:t0 + (ti + 1) * 128, :], o[:], R=[o], W=[ktok_d], owner=o)
                gch = [0, 1, 2, 3] if full else [2, 3]
                def g_item(gc):
                    p = allb[cnt["ab"] % 8]
                    cnt["ab"] += 1
                    for k in range(8):
                        S.op("pe", lambda h: h.matmul(p[:, 0:nt], lhsT=wfm[:, k, 1024 + gc * 128:1024 + (gc + 1) * 128],
                                                      rhs=h_[:, k, c0:c0 + nt], start=(k == 0), stop=(k == 7)),
                             R=[wfm, h_], W=[p])
                    o = gxo[cnt["gxo"] % 2]
                    cnt["gxo"] += 1
                    if gc < 2:
                        evac_copy(o[:, 0:nt], p[:, 0:nt], [p], [o], scale=0.125)
                        S.dma(gqT_d[gc * 128:(gc + 1) * 128, t0:t0 + nt], o[:, 0:nt], R=[o], W=[gqT_d], owner=o)
                    else:
                        evac_copy(o[:, 0:nt], p[:, 0:nt], [p], [o])
                        S.dma(gkT_d[(gc - 2) * 128:(gc - 1) * 128, t0:t0 + nt], o[:, 0:nt], R=[o], W=[gkT_d], owner=o)
                dirs = [0, 1] if blk["need_f"] else [1]
                for d in dirs:
                    pMS = allb[cnt["ab"] % 8]
                    cnt["ab"] += 1
                    for k in range(8):
                        S.op("pe", lambda h: h.matmul(pMS[0:16, 0:nt], lhsT=wfm[:, k, 1536 + 16 * d:1552 + 16 * d],
                                                      rhs=h_[:, k, c0:c0 + nt], start=(k == 0), stop=(k == 7)),
                             R=[wfm, h_], W=[pMS])
                    evac_copy(glrT[d][0:16, 0:nt], pMS[0:16, 0:nt], [pMS], [glrT[d]])
                for gc in gch:
                    g_item(gc)
                def glr_b_item(ti):
                    p = allb[cnt["ab"] % 8]
                    cnt["ab"] += 1
                    for d in dirs:
                        S.op("pe", lambda h: h.matmul(p[:, d * 256:(d + 1) * 256], lhsT=glrT[d][0:17, ti * 128:(ti + 1) * 128],
                                                      rhs=ggw[0:17, d, :], start=True, stop=True),
                             R=[glrT[d], ggw], W=[p])
                    c_lo = dirs[0] * 256
                    S.op("act", lambda h: h.activation(out=lgt[:, c_lo:512], in_=p[:, c_lo:512], func=AF.Exp, scale=-1.0),
                         R=[p], W=[lgt])
                    S.op("act", lambda h: h.activation(out=lgt[:, c_lo:512], in_=lgt[:, c_lo:512], func=AF.Ln, bias=1.0),
                         R=[lgt], W=[lgt])
                    o = lgo[cnt["lgo"] % 2]
                    cnt["lgo"] += 1
                    S.op("dve", lambda h: h.tensor_scalar(out=o[:, c_lo:512], in0=lgt[:, c_lo:512], scalar1=-1.0 / 16.0,
                                                          scalar2=None, op0=ALU.mult), R=[lgt], W=[o])
                    tr = slice(t0 + ti * 128, t0 + (ti + 1) * 128)
                    if blk["need_f"]:
                        S.dma(lgf_d[tr, :], o[:, 0:256], R=[o], W=[lgf_d], owner=o)
                    S.dma(lgb_d[tr, :], o[:, 256:512], R=[o], W=[lgb_d], owner=o)
                groups = [0, 1, 2, 3, 4] if full else [0, 2, 4]
                def tm_item(ti):
                    tr = slice(t0 + ti * 128, t0 + (ti + 1) * 128)
                    lh = lambda k: h_[:, k, c0 + ti * 128:c0 + (ti + 1) * 128]
                    for g in groups:
                        ncol = 512 if g < 4 else 16
                        p = allb[cnt["ab"] % 8]
                        cnt["ab"] += 1
                        for k in range(8):
                            S.op("pe", lambda h: h.matmul(p[:, 0:ncol], lhsT=lh(k), rhs=wtm[:, k, g * 512:g * 512 + ncol],
                                                          start=(k == 0), stop=(k == 7)), R=[h_, wtm], W=[p])
                        if g < 4:
                            o = tmo[g][cnt["tm"] % 2]
                            dst = (v_d, mo_d, gv_d, gr_d)[g]
                            if g == 0 or g == 2:
                                evac_copy(o[:], p[:], [p], [o])
                            elif g == 1:
                                S.op("act", lambda h: h.activation(out=o[:], in_=p[:], func=AF.Sigmoid), R=[p], W=[o])
                            else:
                                S.op("act", lambda h: h.activation(out=o[:], in_=p[:], func=AF.Silu), R=[p], W=[o])
                            S.dma(dst[tr, :], o[:], R=[o], W=[dst], owner=o)
                        else:
                            gt = Gt[cnt["G"] % 2]
                            cnt["G"] += 1
                            S.op("dve", lambda h: h.tensor_tensor(out=gt[:], in0=p[:, 0:16], in1=mgb[:], op=ALU.add),
                                 R=[p, mgb], W=[gt])
                            g3 = gt[:].rearrange("p (a b) -> p a b", b=8)[:, :, 4:8]
                            t3 = Gtmp[:].rearrange("p (a b) -> p a b", b=4)
                            S.op("act", lambda h: h.activation(out=t3, in_=g3, func=AF.Exp, scale=-1.0), R=[gt], W=[Gtmp])
                            S.op("act", lambda h: h.activation(out=t3, in_=t3, func=AF.Ln, bias=1.0), R=[Gtmp], W=[Gtmp])
                            S.op("dve", lambda h: h.tensor_scalar(out=g3, in0=t3, scalar1=-1.0, scalar2=None, op0=ALU.mult),
                                 R=[Gtmp], W=[gt])
                            S.dma(G_d[:, (t0 + ti * 128) // 128, :], gt[:], R=[gt], W=[G_d], owner=gt)
                    cnt["tm"] += 1
                late = []
                for ti in range(nt // 128):
                    late.append(lambda ti=ti: glr_b_item(ti))
                    late.append(lambda ti=ti: ktr_item(ti))
                interleave([(lambda ti=ti: tm_item(ti)) for ti in range(nt // 128)], late)
          S.phase_end()

        if STOP_AFTER >= 3:
          S.phase_begin()
          with contextlib.ExitStack() as es:
            G = sb(es, [128, NCH, 16], F32, "G")
            S.dma(G[:], G_d[:, :, :], R=[G_d], W=[G])
            woutb = sb(es, [128, 8, D], BF16, "woutb")
            g1row = sb(es, [128, D], F32, "g1row")
            dg = sb(es, [128, 128], F32, "dg")
            wstg = [sb(es, [128, D], F32, "wstg") for _ in range(2)]
            PSB = {d_: [ps(es, [128, 512], F32, "psb%d" % d_) for _ in range(4)] for d_ in (1, 0)}
            psi = {}

            def nb(d_, grp=0):
                key = (d_, grp)
                n = psi.get(key, 0)
                psi[key] = n + 1
                return PSB[d_][2 * grp + n % 2]

            def v3(t, n=128):
                return t[:].rearrange("p (a b) -> p a b", b=n)

            def vb3(t):
                return t[:].bitcast(BF16).rearrange("p (a b) -> p a b", b=128)

            gb = [PSB[1][0], PSB[1][1]]
            for k in range(8):
                S.op("dve", lambda h: h.tensor_scalar(out=dg[:], in0=identf[:], scalar1=modc[:, 16 + k, 0:1], scalar2=None,
                                                      op0=ALU.mult), R=[identf, modc], W=[dg])
                S.op("pe", lambda h: h.matmul(gb[k // 4][:, (k % 4) * 128:(k % 4 + 1) * 128], lhsT=onesf[:], rhs=dg[:],
                                              start=True, stop=True), R=[onesf, dg], W=[gb[k // 4]])
            for hf in range(2):
                S.op("act", lambda h: h.activation(out=g1row[:, hf * 512:(hf + 1) * 512], in_=gb[hf][:], func=AF.Copy),
                     R=[gb[hf]], W=[g1row])
            for k in range(8):
                st = wstg[k % 2]
                S.dma(st[:], wout_in[k * 128:(k + 1) * 128, :], W=[st])
                S.op("dve", lambda h: h.scalar_tensor_tensor(out=woutb[:, k, :], in0=st[:], scalar=gncol[:, k:k + 1],
                                                             in1=g1row[:], op0=ALU.mult, op1=ALU.mult),
                     R=[st, gncol, g1row], W=[woutb])

            NG = NCH * 4
            lfc = sb(es, [128, NG], F32, "lfc")
            lic = sb(es, [128, NG], F32, "lic")
            ebc = [sb(es, [128, NG], F32, "ebc") for _ in range(2)]
            Ecol = [sb(es, [128, NG], F32, "Ecol") for _ in range(2)]
            etot = [sb(es, [128, NG], F32, "etot") for _ in range(2)]
            for d in range(2):
                PPg, AAg = PSB[0][0], PSB[0][1]
                S.op("dve", lambda h: h.tensor_copy(out=lfc[:].rearrange("p (c g) -> p c g", g=4), in_=G[:, :, 4 + 8 * d:8 + 8 * d]),
                     R=[G], W=[lfc])
                S.op("dve", lambda h: h.tensor_copy(out=lic[:].rearrange("p (c g) -> p c g", g=4), in_=G[:, :, 8 * d:8 * d + 4]),
                     R=[G], W=[lic])
                S.op("pe", lambda h: h.matmul(PPg[:, 0:NG], lhsT=maskf[d][:], rhs=lfc[:], start=True, stop=True),
                     R=[maskf[d], lfc], W=[PPg])
                S.op("pe", lambda h: h.matmul(AAg[:, 0:NG], lhsT=onesf[:], rhs=lfc[:], start=True, stop=True),
                     R=[onesf, lfc], W=[AAg])
                S.op("act", lambda h: h.activation(out=ebc[d][:], in_=PPg[:, 0:NG], func=AF.Exp), R=[PPg], W=[ebc[d]])
                S.op("dve", lambda h: h.tensor_tensor(out=lic[:], in0=lic[:], in1=PPg[:, 0:NG], op=ALU.subtract),
                     R=[lic, PPg], W=[lic])
                S.op("act", lambda h: h.activation(out=Ecol[d][:], in_=lic[:], func=AF.Exp, bias=lnms[:, 0:1]), R=[lic, lnms], W=[Ecol[d]])
                S.op("act", lambda h: h.activation(out=etot[d][:], in_=AAg[:, 0:NG], func=AF.Exp), R=[AAg], W=[etot[d]])

            class NS:
                pass
            NB = 2
            rowm = sb(es, [128, 2], F32, "rowm")
            S.op("pool", lambda h: h.memset(rowm[:], 0.0), W=[rowm])
            S.op("pool", lambda h: h.memset(rowm[0:64, 0:1], 1.0), W=[rowm])
            S.op("pool", lambda h: h.memset(rowm[64:128, 1:2], 1.0), W=[rowm])
            epsb = sb(es, [128, 1], F32, "epsb")
            S.op("pool", lambda h: h.memset(epsb[:], EPS), W=[epsb])

            def mk_stream(tag):
                T = NS()
                T.qTt = [sb(es, [128, 4, 128], BF16, "qTt" + tag) for _ in range(NB)]
                T.kTt = [sb(es, [128, 4, 128], BF16, "kTt" + tag) for _ in range(NB)]
                T.ktk = [sb(es, [128, 512], BF16, "ktk" + tag) for _ in range(NB)]
                T.vx = [sb(es, [128, 4, 129], BF16, "vx" + tag) for _ in range(NB)]
                T.gqt = [sb(es, [128, 2, 128], BF16, "gqt" + tag) for _ in range(NB)]
                T.gkt = [sb(es, [128, 2, 128], BF16, "gkt" + tag) for _ in range(NB)]
                T.gvt = [sb(es, [128, 512], BF16, "gvt" + tag) for _ in range(NB)]
                T.lgl = [sb(es, [128, 256], F32, "lgl" + tag) for _ in range(NB)]
                T.OTt = [sb(es, [128, D], F32, "OTt" + tag) for _ in range(NB)]
                T.vd = sb(es, [128, 4, 129], BF16, "vd" + tag)
                T.Pm = sb(es, [128, 4, 128], BF16, "Pm" + tag)
                T.Hs = sb(es, [128, 4, 129], F32, "Hs" + tag)
                T.dn = sb(es, [128, 4], F32, "dn" + tag)
                T.dn2 = sb(es, [128, 4], F32, "dn2" + tag)
                T.C32 = sb(es, [128, 4, 129], F32, "C32" + tag)
                T.Cbf = sb(es, [128, 4, 129], BF16, "Cbf" + tag)
                T.S32 = sb(es, [128, 2, 128], F32, "S32" + tag)
                T.ekt = sb(es, [128, 2, 128], F32, "ekt" + tag)
                T.eqt = sb(es, [128, 2, 128], F32, "eqt" + tag)
                T.etg = sb(es, [128, 2], F32, "etg" + tag)
                T.kd = sb(es, [128, 2, 128], BF16, "kd" + tag)
                T.qd = sb(es, [128, 2, 128], BF16, "qd" + tag)
                T.kdt = sb(es, [128, 2, 128], BF16, "kdt" + tag)
                T.Am = sb(es, [128, 4, 128], BF16, "Am" + tag)
                T.kdz = [sb(es, [128, 2, 128], BF16, "kdz" + tag) for _ in range(2)]
                T.Sz = [sb(es, [128, 2, 128], BF16, "Sz" + tag) for _ in range(2)]
                T.etgm = sb(es, [128, 2, 2], F32, "etgm" + tag)
                T.HB = sb(es, [128, D], F32, "HB" + tag)
                T.GT = sb(es, [128, D], BF16, "GT" + tag)
                T.XT = sb(es, [128, D], F32, "XT" + tag)
                T.X1 = sb(es, [128, D], F32, "X1" + tag)
                T.SQ = sb(es, [128, D], F32, "SQ" + tag)
                T.ss8 = sb(es, [128, 8], F32, "ss8" + tag)
                T.rs8 = sb(es, [128, 8], F32, "rs8" + tag)
                T.MX = sb(es, [128, D], BF16, "MX" + tag)
                T.mixT = sb(es, [128, 8, 128], BF16, "mixT" + tag)
                for i in range(NB):
                    S.op("pool", lambda h: h.memset(T.vx[i][:], 1.0), W=[T.vx[i]])
                S.op("pool", lambda h: h.memset(T.C32[:], 0.0), W=[T.C32])
                S.op("pool", lambda h: h.memset(T.Cbf[:], 0.0), W=[T.Cbf])
                S.op("pool", lambda h: h.memset(T.S32[:], 0.0), W=[T.S32])
                for j in range(2):
                    S.op("pool", lambda h: h.memset(T.Sz[j][:], 0.0), W=[T.Sz[j]])
                return T

            ST = {1: mk_stream("b"), 0: mk_stream("f")}

            def load_step(slot, d, ci, with_out, ts=None):
                T = ST[d if ts is None else ts]
                tr = slice(ci * 128, (ci + 1) * 128)
                if with_out:
                    S.dma(T.qTt[slot][:], qT_d[:, tr].rearrange("(h p) t -> p h t", p=128), R=[qT_d], W=[T.qTt[slot]])
                    S.dma(T.kTt[slot][:], kT_d[:, tr].rearrange("(h p) t -> p h t", p=128), R=[kT_d], W=[T.kTt[slot]])
                    S.dma(T.gqt[slot][:], gqT_d[:, tr].rearrange("(g p) t -> p g t", p=128), R=[gqT_d], W=[T.gqt[slot]])
                S.dma(T.ktk[slot][:], ktok_d[tr, :], R=[ktok_d], W=[T.ktk[slot]])
                S.dma(T.vx[slot][:, :, 0:128], v_d[tr, :].rearrange("t (h e) -> t h e", e=128), R=[v_d], W=[T.vx[slot]])
                S.dma(T.gkt[slot][:], gkT_d[:, tr].rearrange("(g p) t -> p g t", p=128), R=[gkT_d], W=[T.gkt[slot]])
                S.dma(T.gvt[slot][:], gv_d[tr, :], R=[gv_d], W=[T.gvt[slot]])
                lsrc = lgf_d if d == 0 else lgb_d
                S.dma(T.lgl[slot][:], lsrc[tr, :], R=[lsrc], W=[T.lgl[slot]])

            def load_combine(d, ci):
                T = ST[d]
                tr = slice(ci * 128, (ci + 1) * 128)
                S.dma(T.HB[:], hb_d[tr, :], R=[hb_d], W=[T.HB])
                S.dma(T.GT[:, 0:512], mo_d[tr, :], R=[mo_d], W=[T.GT])
                S.dma(T.GT[:, 512:1024], gr_d[tr, :], R=[gr_d], W=[T.GT])
                S.dma(T.XT[:], x_in[tr, :], W=[T.XT])

            def mlstm_gen(slot, d, ci, with_out, need_bf=True, ts=None):
                ts = d if ts is None else ts
                T = ST[ts]
                vd, Pm, Hs, dn, dn2, C32, Cbf = T.vd, T.Pm, T.Hs, T.dn, T.dn2, T.C32, T.Cbf
                gs = slice(ci * 4, ci * 4 + 4)
                q_, k_, kt_, vx_ = T.qTt[slot], T.kTt[slot], T.ktk[slot], T.vx[slot]
                OT = T.OTt[slot]
                mk3 = maskf[d][:].unsqueeze(1).broadcast_to([128, 4, 128])
                S.op("dve", lambda h: h.tensor_tensor(out=vd[:], in0=vx_[:], in1=Ecol[d][:, gs].unsqueeze(2).broadcast_to([128, 4, 129]),
                                                       op=ALU.mult), R=[vx_, Ecol[d]], W=[vd])
                yield
                if with_out:
                    PPt = nb(ts)
                    for hh in range(4):
                        S.op("pe", lambda h: h.matmul(v3(PPt)[:, hh, :], lhsT=k_[:, hh, :], rhs=q_[:, hh, :], start=True, stop=True),
                             R=[k_, q_], W=[PPt])
                    yield
                    S.op("dve", lambda h: h.tensor_tensor(out=Pm[:], in0=v3(PPt), in1=mk3, op=ALU.mult),
                         R=[PPt, maskf[d]], W=[Pm])
                    yield
                    Hb_ = [nb(ts), nb(ts)]
                    for hh in range(4):
                        o = Hb_[hh // 2][:, (hh % 2) * 129:(hh % 2 + 1) * 129]
                        S.op("pe", lambda h: h.matmul(o, lhsT=q_[:, hh, :], rhs=Cbf[:, hh, :], start=True, stop=False),
                             R=[q_, Cbf], W=[Hb_[hh // 2]])
                        S.op("pe", lambda h: h.matmul(o, lhsT=Pm[:, hh, :], rhs=vd[:, hh, :], start=False, stop=True),
                             R=[Pm, vd], W=[Hb_[hh // 2]])
                        if hh == 1:
                            yield
                    yield
                    for a in range(2):
                        eb3 = ebc[d][:, ci * 4 + 2 * a:ci * 4 + 2 * a + 2].unsqueeze(2).broadcast_to([128, 2, 129])
                        S.op("dve", lambda h: h.tensor_tensor(out=Hs[:, 2 * a:2 * a + 2, :], in0=Hb_[a][:, 0:258].rearrange("p (b c) -> p b c", c=129),
                                                              in1=eb3, op=ALU.mult), R=[Hb_[a], ebc[d]], W=[Hs])
                        yield
                    S.op("dve", lambda h: h.tensor_scalar(out=dn2[:], in0=Hs[:, :, 128], scalar1=-1.0, scalar2=1.0, op0=ALU.mult, op1=ALU.max),
                         R=[Hs], W=[dn2])
                    yield
                    S.op("dve", lambda h: h.tensor_tensor(out=dn[:], in0=dn2[:], in1=Hs[:, :, 128], op=ALU.max), R=[dn2, Hs], W=[dn])
                    yield
                    S.op("dve", lambda h: h.reciprocal(out=dn[:], in_=dn[:]), R=[dn], W=[dn])
                    yield
                    for hh in range(4):
                        S.op("act", lambda h: h.activation(out=OT[:, hh * 128:(hh + 1) * 128], in_=Hs[:, hh, 0:128], func=AF.Copy,
                                                           scale=dn[:, hh:hh + 1]), R=[Hs, dn], W=[OT])
                    yield
                Ub_ = [nb(ts), nb(ts)]
                for hh in range(4):
                    o = Ub_[hh // 2][:, (hh % 2) * 129:(hh % 2 + 1) * 129]
                    S.op("pe", lambda h: h.matmul(o, lhsT=kt_[:, hh * 128:(hh + 1) * 128], rhs=vd[:, hh, :],
                                                  start=True, stop=True), R=[kt_, vd], W=[Ub_[hh // 2]])
                yield
                for a in range(2):
                    S.op("dve", lambda h: h.tensor_tensor(out=C32[:, 2 * a:2 * a + 2, :], in0=C32[:, 2 * a:2 * a + 2, :],
                                                          in1=Ub_[a][:, 0:258].rearrange("p (b c) -> p b c", c=129), op=ALU.add),
                         R=[C32, Ub_[a]], W=[C32])
                    yield
                et3 = etot[d][:, gs].unsqueeze(2).broadcast_to([128, 4, 129])
                S.op("dve", lambda h: h.tensor_tensor(out=C32[:], in0=C32[:], in1=et3, op=ALU.mult), R=[C32, etot[d]], W=[C32])
                yield
                if need_bf:
                    S.op("act", lambda h: h.activation(out=Cbf[:], in_=C32[:], func=AF.Copy), R=[C32], W=[Cbf])
                    yield

            def gla_gen(slot, d, ci, with_out, need_bf=True, ts=None, acc=None):
                ts = d if ts is None else ts
                T = ST[ts]
                S32, ekt, eqt, etg, kd, qd, kdt, Am, kdz, Sz, etgm = (T.S32, T.ekt, T.eqt, T.etg, T.kd, T.qd, T.kdt, T.Am,
                                                                     T.kdz, T.Sz, T.etgm)
                gq_, gk_, gv_, lg_ = T.gqt[slot], T.gkt[slot], T.gvt[slot], T.lgl[slot]
                OT = T.OTt[slot]
                mk3 = maskf[d][:].unsqueeze(1).broadcast_to([128, 4, 128])
                CUMt = nb(ts, 1)
                C3 = v3(CUMt)
                for g in range(2):
                    S.op("pe", lambda h: h.matmul(C3[:, g, :], lhsT=lg_[:, g * 128:(g + 1) * 128], rhs=maskf[d][:],
                                                  start=True, stop=True), R=[lg_, maskf[d]], W=[CUMt])
                yield
                S.op("act", lambda h: h.activation(out=ekt[:], in_=C3[:, 0:2, :], func=AF.Exp, scale=-1.0), R=[CUMt], W=[ekt])
                tcol = 127 if d == 0 else 0
                S.op("act", lambda h: h.activation(out=etg[:], in_=C3[:, 0:2, tcol], func=AF.Exp), R=[CUMt], W=[etg])
                if acc is not None:
                    S.op("dve", lambda h: h.tensor_tensor(out=acc[:], in0=acc[:], in1=etg[:], op=ALU.mult), R=[acc, etg], W=[acc])
                if with_out:
                    S.op("act", lambda h: h.activation(out=eqt[:], in_=C3[:, 0:2, :], func=AF.Exp), R=[CUMt], W=[eqt])
                yield
                S.op("pool", lambda h: h.tensor_tensor(out=kd[:], in0=gk_[:], in1=ekt[:], op=ALU.mult), R=[gk_, ekt], W=[kd])
                yield
                if with_out:
                    S.op("pool", lambda h: h.tensor_tensor(out=qd[:], in0=gq_[:], in1=eqt[:], op=ALU.mult), R=[gq_, eqt], W=[qd])
                    for j in range(2):
                        S.op("dve", lambda h: h.tensor_scalar(out=kdz[j][:], in0=kd[:], scalar1=rowm[:, j:j + 1], scalar2=None,
                                                              op0=ALU.mult), R=[kd, rowm], W=[kdz[j]])
                    yield
                    AAt = nb(ts, 1)
                    for hh in range(4):
                        g, j = hh // 2, hh % 2
                        S.op("pe", lambda h: h.matmul(v3(AAt)[:, hh, :], lhsT=kdz[j][:, g, :], rhs=qd[:, g, :],
                                                      start=True, stop=True), R=[kdz[j], qd], W=[AAt])
                    yield
                    S.op("dve", lambda h: h.tensor_tensor(out=Am[:], in0=v3(AAt), in1=mk3, op=ALU.mult),
                         R=[AAt, maskf[d]], W=[Am])
                    yield
                    OOt = nb(ts, 1)
                    for hh in range(4):
                        g, j = hh // 2, hh % 2
                        S.op("pe", lambda h: h.matmul(v3(OOt)[:, hh, :], lhsT=qd[:, g, :], rhs=Sz[j][:, g, :],
                                                      start=True, stop=False), R=[qd, Sz[j]], W=[OOt])
                        S.op("pe", lambda h: h.matmul(v3(OOt)[:, hh, :], lhsT=Am[:, hh, :], rhs=gv_[:, hh * 128:(hh + 1) * 128],
                                                      start=False, stop=True), R=[Am, gv_], W=[OOt])
                        if hh == 1:
                            yield
                    yield
                    S.op("act", lambda h: h.activation(out=OT[:, 512:1024], in_=OOt[:], func=AF.Copy), R=[OOt], W=[OT])
                    yield
                Tt = nb(ts, 1)
                for g in range(2):
                    S.op("pe", lambda h: h.transpose(out=vb3(Tt)[:, g, :], in_=kd[:, g, :], identity=identb[:]), R=[kd, identb], W=[Tt])
                yield
                S.op("act", lambda h: h.activation(out=kdt[:], in_=vb3(Tt)[:, 0:2, :], func=AF.Copy), R=[Tt], W=[kdt])
                yield
                UGt = nb(ts, 1)
                for g in range(2):
                    S.op("pe", lambda h: h.matmul(UGt[:, g * 256:(g + 1) * 256], lhsT=kdt[:, g, :], rhs=gv_[:, g * 256:(g + 1) * 256],
                                                  start=True, stop=True), R=[kdt, gv_], W=[UGt])
                yield
                for g in range(2):
                    for j in range(2):
                        ps_ = slice(j * 64, (j + 1) * 64)
                        S.op("dve", lambda h: h.tensor_tensor(out=S32[ps_, g, :], in0=S32[ps_, g, :],
                                                              in1=UGt[ps_, g * 256 + j * 128:g * 256 + (j + 1) * 128], op=ALU.add),
                             R=[S32, UGt], W=[S32])
                    yield
                for g in range(2):
                    S.op("act", lambda h: h.activation(out=S32[:, g, :], in_=S32[:, g, :], func=AF.Copy, scale=etg[:, g:g + 1]),
                         R=[S32, etg], W=[S32])
                yield
                if need_bf:
                    for j in range(2):
                        S.op("act", lambda h: h.activation(out=Sz[j][:], in_=S32[:], func=AF.Copy, scale=rowm[:, j:j + 1]),
                             R=[S32, rowm], W=[Sz[j]])
                    yield

            def out_gen(slot, d, ci, with_out, combine):
                T = ST[d]
                tr = slice(ci * 128, (ci + 1) * 128)
                OT = T.OTt[slot]
                if not with_out:
                    return
                if not combine:
                    S.dma(hb_d[tr, :], OT[:], R=[OT], W=[hb_d], owner=OT)
                    yield
                    return
                HB, GT, XT, X1, SQ, ss8, rs8, MX, mixT = T.HB, T.GT, T.XT, T.X1, T.SQ, T.ss8, T.rs8, T.MX, T.mixT
                S.op("pool", lambda h: h.tensor_tensor(out=HB[:], in0=HB[:], in1=OT[:], op=ALU.add), R=[HB, OT], W=[HB])
                yield
                S.op("dve", lambda h: h.tensor_tensor(out=SQ[:], in0=HB[:], in1=HB[:], op=ALU.mult), R=[HB], W=[SQ])
                yield
                S.op("dve", lambda h: h.tensor_reduce(out=ss8[:], in_=SQ[:].rearrange("p (a b) -> p a b", b=128), axis=AX.X,
                                                      op=ALU.add), R=[SQ], W=[ss8])
                yield
                S.op("act", lambda h: h.activation(out=rs8[:], in_=ss8[:], func=AF.Ln, scale=1.0 / 128.0, bias=epsb[:, 0:1]),
                     R=[ss8, epsb], W=[rs8])
                yield
                S.op("act", lambda h: h.activation(out=rs8[:], in_=rs8[:], func=AF.Exp, scale=-0.5), R=[rs8], W=[rs8])
                yield
                S.op("dve", lambda h: h.tensor_tensor(out=SQ[:].rearrange("p (a b) -> p a b", b=128),
                                                      in0=HB[:].rearrange("p (a b) -> p a b", b=128),
                                                      in1=rs8[:].unsqueeze(2).broadcast_to([128, 8, 128]), op=ALU.mult),
                     R=[HB, rs8], W=[SQ])
                yield
                S.op("pool", lambda h: h.tensor_tensor(out=MX[:], in0=SQ[:], in1=GT[:], op=ALU.mult), R=[SQ, GT], W=[MX])
                yield
                TPt = nb(d, 1)
                for k in range(8):
                    S.op("pe", lambda h: h.transpose(out=vb3(TPt)[:, k, :], in_=MX[:, k * 128:(k + 1) * 128], identity=identb[:]),
                         R=[MX, identb], W=[TPt])
                    if k == 3:
                        yield
                yield
                S.op("act", lambda h: h.activation(out=mixT[:], in_=vb3(TPt), func=AF.Copy), R=[TPt], W=[mixT])
                yield
                for hf in range(2):
                    POt = nb(d, 1)
                    for k in range(8):
                        S.op("pe", lambda h: h.matmul(POt[:], lhsT=mixT[:, k, :], rhs=woutb[:, k, hf * 512:(hf + 1) * 512],
                                                      start=(k == 0), stop=(k == 7)), R=[mixT, woutb], W=[POt])
                        if k == 3:
                            yield
                    yield
                    S.op("dve", lambda h: h.tensor_tensor(out=X1[:, hf * 512:(hf + 1) * 512], in0=XT[:, hf * 512:(hf + 1) * 512],
                                                          in1=POt[:], op=ALU.add), R=[XT, POt], W=[X1])
                    yield
                S.dma(x1_d[tr, :], X1[:], R=[X1], W=[x1_d], owner=X1)
                yield

            def step_gen(slot, d, ci, with_out, combine):
                yield from mlstm_gen(slot, d, ci, with_out)
                yield from gla_gen(slot, d, ci, with_out)
                yield from out_gen(slot, d, ci, with_out, combine)

            def run(gens):
                alive = list(gens)
                while alive:
                    for g in list(alive):
                        try:
                            next(g)
                        except StopIteration:
                            alive.remove(g)

            A_lead = [(65, False), (64, False)] + [(c, False) for c in range(63, 47, -1)]
            S_lead = [(c, False) for c in range(47, 31, -1)]
            pis = sb(es, [128, 2], F32, "pis")
            pic = sb(es, [128, 4], F32, "pic")
            S.op("pool", lambda h: h.memset(pis[:], 1.0), W=[pis])
            S.op("pool", lambda h: h.memset(pic[:], 1.0), W=[pic])
            for (c, _) in S_lead:
                S.op("dve", lambda h: h.tensor_tensor(out=pic[:], in0=pic[:], in1=etot[1][:, c * 4:c * 4 + 4], op=ALU.mult),
                     R=[pic, etot[1]], W=[pic])
            load_step(0, 1, A_lead[0][0], False, ts=1)
            load_step(0, 1, S_lead[0][0], False, ts=0)
            for i in range(len(A_lead)):
                if i + 1 < len(A_lead):
                    load_step((i + 1) % NB, 1, A_lead[i + 1][0], False, ts=1)
                if i + 1 < len(S_lead):
                    load_step((i + 1) % NB, 1, S_lead[i + 1][0], False, ts=0)
                gens = [mlstm_gen(i % NB, 1, A_lead[i][0], False, False, ts=1), gla_gen(i % NB, 1, A_lead[i][0], False, False, ts=1)]
                if i < len(S_lead):
                    gens += [mlstm_gen(i % NB, 1, S_lead[i][0], False, False, ts=0),
                             gla_gen(i % NB, 1, S_lead[i][0], False, False, ts=0, acc=pis)]
                run(gens)
            TA, TS_ = ST[1], ST[0]
            S.op("dve", lambda h: h.tensor_tensor(out=TA.C32[:], in0=TA.C32[:], in1=pic[:].unsqueeze(2).broadcast_to([128, 4, 129]),
                                                  op=ALU.mult), R=[TA.C32, pic], W=[TA.C32])
            S.op("dve", lambda h: h.tensor_tensor(out=TA.C32[:], in0=TA.C32[:], in1=TS_.C32[:], op=ALU.add),
                 R=[TA.C32, TS_.C32], W=[TA.C32])
            S.op("act", lambda h: h.activation(out=TA.Cbf[:], in_=TA.C32[:], func=AF.Copy), R=[TA.C32], W=[TA.Cbf])
            for g in range(2):
                S.op("dve", lambda h: h.scalar_tensor_tensor(out=TA.S32[:, g, :], in0=TA.S32[:, g, :], scalar=pis[:, g:g + 1],
                                                             in1=TS_.S32[:, g, :], op0=ALU.mult, op1=ALU.add),
                     R=[TA.S32, pis, TS_.S32], W=[TA.S32])
                for j in range(2):
                    S.op("act", lambda h: h.activation(out=TA.Sz[j][:, g, :], in_=TA.S32[:, g, :], func=AF.Copy,
                                                       scale=rowm[:, j:j + 1]), R=[TA.S32, rowm], W=[TA.Sz[j]])
            S.op("pool", lambda h: h.memset(TS_.C32[:], 0.0), W=[TS_.C32])
            S.op("pool", lambda h: h.memset(TS_.S32[:], 0.0), W=[TS_.S32])

            seqB = [(64, False), (65, False)] + [(c, True) for c in range(0, 32)]
            seqA = [None, None] + [(c, True) for c in range(31, -1, -1)]
            nI = len(seqB)
            combA = lambda c: c <= 15
            combB = lambda c: c >= 16

            def chain(*gs):
                for g_ in gs:
                    if g_ is not None:
                        yield from g_

            load_step(0, 0, seqB[0][0], seqB[0][1])
            for i in range(nI + 1):
                a = seqA[i] if i < nI else None
                b = seqB[i] if i < nI else None
                an = seqA[i + 1] if i + 1 < nI else None
                bn = seqB[i + 1] if i + 1 < nI else None
                ap = seqA[i - 1] if i - 1 >= 0 else None
                bp = seqB[i - 1] if i - 1 >= 0 else None
                if an is not None:
                    load_step((i + 1) % NB, 1, an[0], an[1])
                if bn is not None:
                    load_step((i + 1) % NB, 0, bn[0], bn[1])
                gens = []
                oA = out_gen((i - 1) % NB, 1, ap[0], ap[1], combA(ap[0])) if ap is not None else None
                oB = out_gen((i - 1) % NB, 0, bp[0], bp[1], combB(bp[0])) if bp is not None else None
                if a is not None:
                    nbfA = an is not None and an[1]
                    gens += [mlstm_gen(i % NB, 1, a[0], a[1], nbfA), chain(gla_gen(i % NB, 1, a[0], a[1], nbfA), oA)]
                elif oA is not None:
                    gens.append(oA)
                if b is not None:
                    nbfB = bn is not None and bn[1]
                    gens += [mlstm_gen(i % NB, 0, b[0], b[1], nbfB), chain(gla_gen(i % NB, 0, b[0], b[1], nbfB), oB)]
                elif oB is not None:
                    gens.append(oB)
                run(gens)
                if a is not None and a[1] and combA(a[0]):
                    load_combine(1, a[0])
                if b is not None and b[1] and combB(b[0]):
                    load_combine(0, b[0])
          S.phase_end()

        if STOP_AFTER >= 4:
          S.phase_begin()
          with contextlib.ExitStack() as es:
            wgu = sb(es, [128, 8, 2 * DFF], BF16, "wgu")
            wdn = sb(es, [128, 22, D], BF16, "wdn")
            gfin = sb(es, [128, D], F32, "gfin")
            PW = 1408
            TPl = [ps(es, [128, 512], F32, "TPb") for _ in range(2)]

            def bfv4(t):
                return t[:].bitcast(BF16).rearrange("p (a b) -> p a b", b=128)
            PSa = [ps(es, [128, 512], F32, "PSa") for _ in range(2)]
            PSg = [ps(es, [128, 512], F32, "PSg") for _ in range(2)]
            PSdl = [ps(es, [128, 512], F32, "PSd") for _ in range(2)]
            allb4 = [PSa[0], PSg[0], PSdl[0], TPl[0], PSa[1], PSg[1], PSdl[1], TPl[1]]
            cnt4 = [0]

            def nb4():
                t = allb4[cnt4[0] % 8]
                cnt4[0] += 1
                return t
            S.dma(gfin[:], gfin_in.broadcast_to([128, D]), W=[gfin])
            with contextlib.ExitStack() as ses:
                NST = 6
                stg2 = [sb(ses, [128, PW], F32, "stg2") for _ in range(NST)]
                g2row = sb(ses, [128, D], F32, "g2row")
                dg4 = sb(ses, [128, 128], F32, "dg4")
                for k in range(8):
                    S.op("dve", lambda h: h.tensor_scalar(out=dg4[:], in0=identf[:], scalar1=modc[:, 40 + k, 0:1], scalar2=None,
                                                          op0=ALU.mult), R=[identf, modc], W=[dg4])
                    S.op("pe", lambda h: h.matmul(PSdl[k // 4][:, (k % 4) * 128:(k % 4 + 1) * 128], lhsT=onesf[:], rhs=dg4[:], start=True, stop=True),
                         R=[onesf, dg4], W=[PSdl[k // 4]])
                for hf in range(2):
                    S.op("act", lambda h: h.activation(out=g2row[:, hf * 512:(hf + 1) * 512], in_=PSdl[hf][:],
                                                       func=AF.Copy), R=[PSdl[hf]], W=[g2row])
                ci_ = 0
                for fc in range(22):
                    st = stg2[ci_ % NST]
                    ci_ += 1
                    S.dma(st[:, 0:D], wdn_in[fc * 128:(fc + 1) * 128, :], W=[st])
                    S.op("dve", lambda h: h.tensor_tensor(out=wdn[:, fc, :], in0=st[:, 0:D], in1=g2row[:], op=ALU.mult),
                         R=[st, g2row], W=[wdn])
                for k in range(8):
                    for pc in range(4):
                        st = stg2[ci_ % NST]
                        ci_ += 1
                        S.dma(st[:, :], wgu_in[k * 128:(k + 1) * 128, pc * PW:(pc + 1) * PW], W=[st])
                        o_ = wgu[:, k, pc * PW:(pc + 1) * PW]
                        if ci_ % 3 == 0:
                            S.op("dve", lambda h: h.tensor_copy(out=o_, in_=st[:, :]), R=[st], W=[wgu])
                        elif ci_ % 3 == 1:
                            S.op("pool", lambda h: h.tensor_copy(out=o_, in_=st[:, :]), R=[st], W=[wgu])
                        else:
                            S.op("act", lambda h: h.activation(out=o_, in_=st[:, :], func=AF.Copy), R=[st], W=[wgu])
            S.phase_end()
            x1b = sb(es, [128, 2, D], F32, "x1b")
            h2T = sb(es, [128, 8, 256], BF16, "h2T")
            uT = sb(es, [128, 22, 256], BF16, "uT")
            sa = [sb(es, [128, 256], F32, "sa") for _ in range(2)]
            x2 = [sb(es, [128, D], F32, "x2") for _ in range(2)]
            xn4 = [sb(es, [128, D], BF16, "xn4") for _ in range(2)]
            junk4 = sb(es, [128, D], BF16, "junk4")
            ss4 = [sb(es, [128, 2], F32, "ss4") for _ in range(2)]
            rs4 = [sb(es, [128, 2], F32, "rs4") for _ in range(2)]

            nblk = OWN // 256
            xn4s = [[xn4[0], xn4[1]], [sb(es, [128, D], BF16, "xn4c"), sb(es, [128, D], BF16, "xn4d")]]
            h2Ts = [h2T, sb(es, [128, 8, 256], BF16, "h2Tb")]
            ssf = [sb(es, [128, 2], F32, "ssf") for _ in range(2)]
            rsf = [sb(es, [128, 2], F32, "rsf") for _ in range(2)]

            def p4a_pieces(bi):
                ssl, rsl = ss4[bi % 2], rs4[bi % 2]
                def sq(sub):
                    rows = slice(bi * 256 + sub * 128, bi * 256 + (sub + 1) * 128)
                    S.dma(x1b[:, sub, :], x1_d[rows, :], R=[x1_d], W=[x1b])
                    S.op("act", lambda h: h.activation(out=xn4s[bi % 2][sub][:], in_=x1b[:, sub, :], func=AF.Square,
                                                       accum_out=ssl[:, sub:sub + 1]), R=[x1b], W=[xn4s[bi % 2][sub], ssl])
                def cp(sub):
                    S.op("act", lambda h: h.activation(out=xn4s[bi % 2][sub][:], in_=x1b[:, sub, :], func=AF.Copy,
                                                       scale=rsl[:, sub:sub + 1]), R=[x1b, rsl], W=[xn4s[bi % 2][sub]])
                return [lambda: sq(0), lambda: sq(1), lambda: rstd_from_ss(ssl, rsl, 2, 1.0 / D), lambda: cp(0), lambda: cp(1)]

            def p4b(bi):
                hh_ = h2Ts[bi % 2]
                for sub in range(2):
                    xnl = xn4s[bi % 2][sub]
                    TPb = nb4()
                    for k in range(8):
                        S.op("pe", lambda h: h.transpose(out=bfv4(TPb)[:, k, :], in_=xnl[:, k * 128:(k + 1) * 128], identity=identb[:]),
                             R=[xnl, identb], W=[TPb])
                    for k in range(8):
                        o = hh_[:, k, sub * 128:(sub + 1) * 128]
                        if k != 7:
                            S.op("dve", lambda h: h.tensor_scalar(out=o, in0=bfv4(TPb)[:, k, :], scalar1=A2[:, k:k + 1],
                                                                  scalar2=modc[:, 24 + k, 0:1], op0=ALU.mult, op1=ALU.add),
                                 R=[TPb, A2, modc], W=[hh_])
                        else:
                            S.op("act", lambda h: h.activation(out=o, in_=bfv4(TPb)[:, k, :], func=AF.Identity, scale=A2[:, k:k + 1],
                                                               bias=modc[:, 24 + k, 0:1]), R=[TPb, A2, modc], W=[hh_])

            for pc_ in p4a_pieces(0):
                pc_()
            p4b(0)
            xi = 0
            for bi in range(nblk):
                hcur = h2Ts[bi % 2]
                pieces = p4a_pieces(bi + 1) if bi + 1 < nblk else []
                for fc in range(22):
                    pa, pg, sal = nb4(), nb4(), sa[fc % 2]
                    for k in range(8):
                        S.op("pe", lambda h: h.matmul(pa[:, 0:256], lhsT=wgu[:, k, fc * 128:(fc + 1) * 128], rhs=hcur[:, k, :],
                                                      start=(k == 0), stop=(k == 7)), R=[wgu, hcur], W=[pa])
                    for k in range(8):
                        S.op("pe", lambda h: h.matmul(pg[:, 0:256], lhsT=wgu[:, k, DFF + fc * 128:DFF + (fc + 1) * 128], rhs=hcur[:, k, :],
                                                      start=(k == 0), stop=(k == 7)), R=[wgu, hcur], W=[pg])
                    S.op("act", lambda h: h.activation(out=sal[:], in_=pa[:, 0:256], func=AF.Silu), R=[pa], W=[sal])
                    S.op("dve", lambda h: h.tensor_tensor(out=uT[:, fc, :], in0=sal[:], in1=pg[:, 0:256], op=ALU.mult),
                         R=[sal, pg], W=[uT])
                    if pieces and fc >= 2 and fc % 2 == 0 and (fc - 2) // 2 < len(pieces):
                        pieces[(fc - 2) // 2]()
                if bi + 1 < nblk:
                    p4b(bi + 1)
                for sub in range(2):
                    rows = slice(bi * 256 + sub * 128, bi * 256 + (sub + 1) * 128)
                    x2l = x2[xi % 2]
                    ssl, rsl = ssf[xi % 2], rsf[xi % 2]
                    xi += 1
                    S.dma(x2l[:], x1_d[rows, :], R=[x1_d], W=[x2l])
                    for hf in range(2):
                        pd = nb4()
                        for fc in range(22):
                            S.op("pe", lambda h: h.matmul(pd[:], lhsT=uT[:, fc, sub * 128:(sub + 1) * 128],
                                                          rhs=wdn[:, fc, hf * 512:(hf + 1) * 512], start=(fc == 0), stop=(fc == 21)),
                                 R=[uT, wdn], W=[pd])
                        S.op("dve", lambda h: h.tensor_tensor(out=x2l[:, hf * 512:(hf + 1) * 512], in0=x2l[:, hf * 512:(hf + 1) * 512],
                                                              in1=pd[:], op=ALU.add), R=[x2l, pd], W=[x2l])
                    S.op("act", lambda h: h.activation(out=junk4[:], in_=x2l[:], func=AF.Square, accum_out=ssl[:, 0:1]),
                         R=[x2l], W=[junk4, ssl])
                    rstd_from_ss(ssl, rsl, 1, 1.0 / D)
                    S.op("dve", lambda h: h.scalar_tensor_tensor(out=x2l[:], in0=x2l[:], scalar=rsl[:, 0:1], in1=gfin[:],
                                                                 op0=ALU.mult, op1=ALU.mult), R=[x2l, rsl, gfin], W=[x2l])
                    S.dma(y_out[rows, :], x2l[:], R=[x2l], W=[y_buf], owner=x2l)
          S.barrier()

        S.barrier()
    return nc


def _prep_inputs(x, c, ctx, c_ctx, w_ada, b_ada, g_mix, w_in, conv_w, m_gate_b, m_norm_g, g_gate_w, g_gate_b,
                 g_norm_g, w_out, g_ffn, w_gu, w_down, g_final):
    f = np.float32

    def col(v, n):
        return np.ascontiguousarray(np.asarray(v, f).reshape(n, 128).T)

    w_in0 = np.asarray(w_in[0], f)
    sizes = (1024, 512, 512, 16, 256, 256, 512, 512, 32)
    offs = np.cumsum((0,) + sizes)
    mqk, mv, mo, mg, gq, gk, gv, gr, glr = [w_in0[:, offs[i]:offs[i + 1]] for i in range(9)]
    shared = dict(
        w_ada=np.ascontiguousarray(np.asarray(w_ada[0], f)),
        b_ada_c=col(b_ada[0], 48), gmix_c=col(g_mix[0], 8), gffn_c=col(g_ffn[0], 8),
        gfin_r=np.ascontiguousarray(np.asarray(g_final, f).reshape(1, D)),
        gn_c=col(np.concatenate([np.asarray(m_norm_g[0], f), np.asarray(g_norm_g[0], f)]), 8),
        w_out=np.ascontiguousarray(np.asarray(w_out[0], f)),
        w_gu=np.ascontiguousarray(np.asarray(w_gu[0], f)),
        w_down=np.ascontiguousarray(np.asarray(w_down[0], f)),
    )
    per_mirror = []
    for mir in range(2):
        if mir == 0:
            mg_l, glr_l = mg, glr
            mgb = np.asarray(m_gate_b[0], f)
            ggw = np.asarray(g_gate_w[0], f)
            ggb = np.asarray(g_gate_b[0], f)
            cw = np.asarray(conv_w[0], f)
        else:
            mg_l = np.concatenate([mg[:, 8:16], mg[:, 0:8]], 1)
            glr_l = np.concatenate([glr[:, 16:32], glr[:, 0:16]], 1)
            mgb = np.concatenate([np.asarray(m_gate_b[0], f)[8:16], np.asarray(m_gate_b[0], f)[0:8]])
            ggw = np.asarray(g_gate_w[0], f)[::-1]
            ggb = np.asarray(g_gate_b[0], f)[::-1]
            cw = np.asarray(conv_w[0], f)[::-1, ::-1]
        w_fm = np.ascontiguousarray(np.concatenate([mqk, gq, gk, glr_l], 1))
        w_tm = np.ascontiguousarray(np.concatenate([mv, mo, gv, gr, mg_l], 1))
        convw_c = np.ascontiguousarray(cw.reshape(9, 8, 128).transpose(2, 0, 1))
        ggw_e = np.ascontiguousarray(np.concatenate([ggw, ggb[:, None, :]], 1))
        per_mirror.append(dict(w_fm=w_fm, w_tm=w_tm, convw_c=convw_c, mgb_r=np.ascontiguousarray(mgb.reshape(1, 16)),
                               ggw_e=ggw_e))
    in_maps = []
    for core in range(8):
        b, mir = core // 2, core % 2
        xl = np.asarray(x[b], f)
        cl = np.asarray(ctx[b], f)
        if mir:
            xl = xl[::-1]
            cl = cl[::-1]
        cvec = np.stack([col(c[b], 8), col(c_ctx, 8)], -1)
        m = dict(x_l=np.ascontiguousarray(xl), ctx_l=np.ascontiguousarray(cl), cvec=np.ascontiguousarray(cvec))
        m.update(shared)
        m.update(per_mirror[mir])
        in_maps.append(m)
    return in_maps


_NC_CACHE = {}


def kernel(**inputs):
    in_maps = _prep_inputs(**inputs)
    if "nc" not in _NC_CACHE:
        _NC_CACHE["nc"] = build_program()
    nc = _NC_CACHE["nc"]
    res = run_bass_kernel_spmd(nc, in_maps, core_ids=list(range(8)))
    out = np.zeros((4, SEQ, D), np.float32)
    for core in range(8):
        b, mir = core // 2, core % 2
        y = np.asarray(res.results[core]["y"])
        if mir:
            out[b, OWN:] = y[::-1]
        else:
            out[b, :OWN] = y
    if DEBUG:
        kernel.last = res
    return out
```

```python
import contextlib
import math
import os
import numpy as np
import concourse.bass as bass
import concourse.mybir as mybir
from concourse.bass_utils import run_bass_kernel_spmd

F32 = mybir.dt.float32
BF16 = mybir.dt.bfloat16
AF = mybir.ActivationFunctionType
ALU = mybir.AluOpType
AX = mybir.AxisListType

D = 1024
SEQ = 8192
OWN = 4096
CTX = 256
NTOK = SEQ + CTX
NCH = NTOK // 128
DFF = 2816
EPS = 1e-6
FM_COLS = 1568
TM_COLS = 2064
LN_MSCALE = math.log(128.0 ** -0.5)

DEBUG = bool(int(os.environ.get("KDEBUG", "0")))
STOP_AFTER = int(os.environ.get("KSTOP", "9"))


class Buf:
    __slots__ = ("name", "w", "r", "dsem", "psum")

    def __init__(self, name):
        self.name = name
        self.psum = False
        self.w = {}
        self.r = {}
        self.dsem = None


class Tl:
    def __init__(self, t, name):
        self.t = t
        self.b = Buf(name)

    def __getitem__(self, k):
        return self.t[k]


class Sched:
    def __init__(self, nc, es):
        self.nc = nc
        self.es = es
        self.eng = {}
        for n, h in (("pe", nc.tensor), ("dve", nc.vector), ("act", nc.scalar),
                     ("pool", nc.gpsimd), ("sp", nc.sync)):
            sem = es.enter_context(nc.semaphore("s_" + n))
            self.eng[n] = dict(h=h, sem=sem, cnt=0, waited={})
        self.dsems = []
        self.nops = 0
        self.free = []
        self.phase_sems = None

    def phase_begin(self):
        self.phase_sems = []

    def phase_end(self):
        self.barrier()
        if self.phase_sems:
            self.free += self.phase_sems
        self.phase_sems = None

    def _wait(self, en, evs):
        e = self.eng[en]
        need = {}
        for (sem, val) in evs:
            k = id(sem)
            if k not in need or need[k][1] < val:
                need[k] = (sem, val)
        for k, (sem, val) in need.items():
            if e["waited"].get(k, 0) < val:
                e["h"].wait_ge(sem, val)
                e["waited"][k] = val

    def _deps(self, en, R, W, skip_own):
        own = id(self.eng[en]["sem"])
        evs = []
        for b in R:
            b = b.b if isinstance(b, Tl) else b
            for k, ev in b.w.items():
                if skip_own and k == own:
                    continue
                evs.append(ev)
            if b.psum:
                for k, ev in b.r.items():
                    if k != own:
                        evs.append(ev)
        for b in W:
            b = b.b if isinstance(b, Tl) else b
            for k, ev in b.w.items():
                if k != own:
                    evs.append(ev)
            for k, ev in b.r.items():
                if k != own:
                    evs.append(ev)
        return evs

    def _record(self, ev, R, W):
        k = id(ev[0])
        for b in R:
            b = b.b if isinstance(b, Tl) else b
            b.r[k] = ev
        for b in W:
            b = b.b if isinstance(b, Tl) else b
            b.w[k] = ev

    def op(self, en, fn, R=(), W=()):
        e = self.eng[en]
        self._wait(en, self._deps(en, R, W, skip_own=(en == "pe")))
        ins = fn(e["h"])
        ins.then_inc(e["sem"], 1)
        e["cnt"] += 1
        self._record((e["sem"], e["cnt"]), R, W)
        self.nops += 1

    def dma(self, out, in_, R=(), W=(), owner=None, q="sp"):
        owner = owner if owner is not None else W[0]
        ob = owner.b if isinstance(owner, Tl) else owner
        if ob.dsem is None:
            if self.free:
                ob.dsem = self.free.pop()
            else:
                ob.dsem = [self.es.enter_context(self.nc.semaphore("d_%d" % len(self.dsems))), 0]
                self.dsems.append(ob.dsem)
            if self.phase_sems is not None:
                self.phase_sems.append(ob.dsem)
        e = self.eng[q]
        self._wait(q, self._deps(q, R, W, skip_own=False))
        ins = e["h"].dma_start(out=out, in_=in_)
        ob.dsem[1] += 16
        ins.then_inc(ob.dsem[0], 16)
        self._record((ob.dsem[0], ob.dsem[1]), R, W)
        self.nops += 1

    def all_events(self):
        evs = [(e["sem"], e["cnt"]) for e in self.eng.values() if e["cnt"] > 0]
        evs += [(d[0], d[1]) for d in self.dsems if d[1] > 0]
        return evs

    def barrier(self):
        evs = self.all_events()
        for en in self.eng:
            self._wait(en, evs)


def build_program():
    nc = bass.Bass("TRN2", target_bir_lowering=False)

    def din(name, shape, dt=F32):
        return nc.dram_tensor(name, list(shape), dt, kind="ExternalInput").ap()

    def dscr(name, shape, dt):
        kind = "ExternalOutput" if DEBUG else "Internal"
        return Tl(nc.dram_tensor(name, list(shape), dt, kind=kind).ap(), name)

    x_in = din("x_l", [SEQ, D])
    ctx_in = din("ctx_l", [CTX, D])
    cvec_in = din("cvec", [128, 8, 2])
    wada_in = din("w_ada", [D, 6 * D])
    bada_in = din("b_ada_c", [128, 48])
    gmix_in = din("gmix_c", [128, 8])
    gffn_in = din("gffn_c", [128, 8])
    gfin_in = din("gfin_r", [1, D])
    wfm_in = din("w_fm", [D, FM_COLS])
    wtm_in = din("w_tm", [D, TM_COLS])
    convw_in = din("convw_c", [128, 9, 8])
    mgb_in = din("mgb_r", [1, 16])
    ggw_in = din("ggw_e", [2, 17, 256])
    gn_in = din("gn_c", [128, 8])
    wout_in = din("w_out", [D, D])
    wgu_in = din("w_gu", [D, 2 * DFF])
    wdn_in = din("w_down", [DFF, D])
    y_out = nc.dram_tensor("y", [OWN, D], F32, kind="ExternalOutput").ap()
    y_buf = Buf("y")

    qT_d = dscr("qT_d", [512, OWN], BF16)
    kT_d = dscr("kT_d", [512, OWN], BF16)
    ktok_d = dscr("ktok_d", [NTOK, 512], BF16)
    v_d = dscr("v_d", [NTOK, 512], BF16)
    mo_d = dscr("mo_d", [OWN, 512], BF16)
    G_d = dscr("G_d", [128, NCH, 16], F32)
    gqT_d = dscr("gqT_d", [256, OWN], BF16)
    gkT_d = dscr("gkT_d", [256, NTOK], BF16)
    gv_d = dscr("gv_d", [NTOK, 512], BF16)
    gr_d = dscr("gr_d", [OWN, 512], BF16)
    lgf_d = dscr("lgf_d", [NTOK, 256], F32)
    lgb_d = dscr("lgb_d", [NTOK, 256], F32)
    hb_d = dscr("hb_d", [OWN, D], F32)
    x1_d = dscr("x1_d", [OWN, D], F32)

    dbgC, dbgS, dbgCb = {}, {}, Buf("dbg")
    if DEBUG:
        for nm in ("cb", "cf", "ob"):
            dbgC[nm] = nc.dram_tensor("dbgC_" + nm, [128, 516], F32, kind="ExternalOutput").ap()
            dbgS[nm] = nc.dram_tensor("dbgS_" + nm, [128, 256], F32, kind="ExternalOutput").ap()

    with contextlib.ExitStack() as ges:
        S = Sched(nc, ges)
        uid = [0]

        def sb(es, shape, dt, name=None):
            uid[0] += 1
            nm = (name or "t") + "_%d" % uid[0]
            return Tl(es.enter_context(nc.sbuf_tensor(nm, list(shape), dt)), nm)

        def ps(es, shape, dt, name=None):
            uid[0] += 1
            nm = (name or "p") + "_%d" % uid[0]
            t = Tl(es.enter_context(nc.psum_tensor(nm, list(shape), dt)), nm)
            t.b.psum = True
            return t

        identb = sb(ges, [128, 128], BF16, "identb")
        identf = sb(ges, [128, 128], F32, "identf")
        onesf = sb(ges, [128, 128], F32, "onesf")
        maskf = [sb(ges, [128, 128], F32, "mask%d" % d) for d in range(2)]
        modc = sb(ges, [128, 48, 2], F32, "modc")
        A1 = sb(ges, [128, 8, 2], F32, "A1")
        A2 = sb(ges, [128, 8], F32, "A2")
        gmix = sb(ges, [128, 8], F32, "gmix")
        gffn = sb(ges, [128, 8], F32, "gffn")
        gncol = sb(ges, [128, 8], F32, "gncol")
        lnms = sb(ges, [128, 1], F32, "lnms")
        epsg = sb(ges, [128, 1], F32, "epsg")
        S.op("pool", lambda h: h.memset(epsg[:], EPS), W=[epsg])
        S.op("pool", lambda h: h.memset(lnms[:], LN_MSCALE), W=[lnms])

        S.op("pool", lambda h: h.memset(onesf[:], 1.0), W=[onesf])
        S.op("pool", lambda h: h.memset(identf[:], 1.0), W=[identf])
        S.op("pool", lambda h: h.affine_select(out=identf[:], in_=identf[:], pattern=[[-1, 128]],
                                               compare_op=ALU.is_equal, fill=0.0, base=0,
                                               channel_multiplier=1), R=[identf], W=[identf])
        S.op("dve", lambda h: h.tensor_copy(out=identb[:], in_=identf[:]), R=[identf], W=[identb])
        for d in range(2):
            S.op("pool", lambda h: h.memset(maskf[d][:], 1.0), W=[maskf[d]])
            sg = 1 if d == 0 else -1
            S.op("pool", lambda h: h.affine_select(out=maskf[d][:], in_=maskf[d][:], pattern=[[sg, 128]],
                                                   compare_op=ALU.is_ge, fill=0.0, base=0,
                                                   channel_multiplier=-sg), R=[maskf[d]], W=[maskf[d]])
        S.dma(gmix[:], gmix_in[:, :], W=[gmix])
        S.dma(gffn[:], gffn_in[:, :], W=[gffn])
        S.dma(gncol[:], gn_in[:, :], W=[gncol])

        S.phase_begin()
        with contextlib.ExitStack() as es:
            cv = sb(es, [128, 8, 2], F32, "cv")
            sc = sb(es, [128, 8, 2], F32, "sc")
            bad = sb(es, [128, 48], F32, "bad")
            wst = [sb(es, [128, 6 * D], F32, "wst") for _ in range(4)]
            pm = ps(es, [128, 48, 2], F32, "pm")
            S.dma(cv[:], cvec_in[:, :, :], W=[cv])
            S.dma(bad[:], bada_in[:, :], W=[bad])
            S.op("act", lambda h: h.activation(out=sc[:], in_=cv[:], func=AF.Silu), R=[cv], W=[sc])
            S.op("dve", lambda h: h.memset(pm[:], 0.0), W=[pm])
            for k in range(8):
                w = wst[k % 4]
                for q4 in range(4):
                    S.dma(w[:, q4 * 1536:(q4 + 1) * 1536], wada_in[k * 128:(k + 1) * 128, q4 * 1536:(q4 + 1) * 1536], W=[w])
                for j in range(48):
                    S.op("pe", lambda h: h.matmul(pm[:, j, :], lhsT=w[:, j * 128:(j + 1) * 128], rhs=sc[:, k, :],
                                                  start=False, stop=(k == 7), skip_group_check=True), R=[w, sc], W=[pm])
            for r in range(2):
                S.op("dve", lambda h: h.tensor_tensor(out=modc[:, :, r], in0=pm[:, :, r], in1=bad[:],
                                                      op=ALU.add), R=[pm, bad], W=[modc])
            for r in range(2):
                S.op("dve", lambda h: h.scalar_tensor_tensor(out=A1[:, :, r], in0=modc[:, 8:16, r], scalar=1.0,
                                                             in1=gmix[:], op0=ALU.add, op1=ALU.mult),
                     R=[modc, gmix], W=[A1])
            S.op("dve", lambda h: h.scalar_tensor_tensor(out=A2[:], in0=modc[:, 32:40, 0], scalar=1.0,
                                                         in1=gffn[:], op0=ALU.add, op1=ALU.mult),
                 R=[modc, gffn], W=[A2])
        S.phase_end()

        def B1(r, k):
            return modc[:, k, r:r + 1]

        def rstd_from_ss(ss, rs, n, scale):
            S.op("act", lambda h: h.activation(out=rs[:, 0:n], in_=ss[:, 0:n], func=AF.Ln, scale=scale, bias=epsg[:, 0:1]),
                 R=[ss, epsg], W=[rs])
            S.op("act", lambda h: h.activation(out=rs[:, 0:n], in_=rs[:, 0:n], func=AF.Exp, scale=-0.5), R=[rs], W=[rs])

        if STOP_AFTER >= 1 and not os.environ.get('KSKIP1'):
          S.phase_begin()
          with contextlib.ExitStack() as es:
            wfm = sb(es, [128, 8, FM_COLS], BF16, "wfm")
            wtm = sb(es, [128, 8, TM_COLS], BF16, "wtm")
            dgw = sb(es, [128, 72, 128], BF16, "dgw")
            cw = sb(es, [128, 9, 8], F32, "cw")
            ggwf = sb(es, [17, 2, 256], F32, "ggwf")
            ggw = sb(es, [17, 2, 256], BF16, "ggw")
            mgb = sb(es, [128, 16], F32, "mgb")
            S.dma(cw[:], convw_in[:, :, :], W=[cw])
            S.dma(mgb[:], mgb_in.broadcast_to([128, 16]), W=[mgb])
            S.dma(ggwf[:], ggw_in.rearrange("d k n -> k d n"), W=[ggwf])
            S.op("dve", lambda h: h.tensor_copy(out=ggw[:], in_=ggwf[:]), R=[ggwf], W=[ggw])
            with contextlib.ExitStack() as ses1:
                stg = [sb(ses1, [128, TM_COLS], F32, "stg") for _ in range(5)]
                ci1 = 0
                engs = ("dve", "pool", "act")
                for k in range(8):
                    for (dst, src, ncol) in ((wfm, wfm_in, FM_COLS), (wtm, wtm_in, TM_COLS)):
                        st = stg[ci1 % 5]
                        en = engs[ci1 % 3]
                        ci1 += 1
                        S.dma(st[:, 0:ncol], src[k * 128:(k + 1) * 128, :], W=[st])
                        if en == "act":
                            S.op("act", lambda h: h.activation(out=dst[:, k, :], in_=st[:, 0:ncol], func=AF.Copy), R=[st], W=[dst])
                        else:
                            S.op(en, lambda h: h.tensor_copy(out=dst[:, k, :], in_=st[:, 0:ncol]), R=[st], W=[dst])
            S.barrier()
            for tap in range(9):
                for ch in range(8):
                    S.op("dve", lambda h: h.tensor_scalar(out=dgw[:, tap * 8 + ch, :], in0=identf[:],
                                                          scalar1=cw[:, tap, ch:ch + 1], scalar2=None,
                                                          op0=ALU.mult), R=[identf, cw], W=[dgw])

            xt = [sb(es, [128, D], F32, "xt") for _ in range(6)]
            xn = [[sb(es, [128, D], BF16, "xn") for _ in range(5)] for _ in range(2)]
            ssq = [sb(es, [128, 8], F32, "ssq") for _ in range(2)]
            rsq = [sb(es, [128, 8], F32, "rsq") for _ in range(2)]
            hT = [sb(es, [128, 8, 640], BF16, "hT") for _ in range(2)]
            zT = sb(es, [128, 8, 640], BF16, "zT")
            qk = [sb(es, [128, 512], BF16, "qk") for _ in range(3)]
            kTs = sb(es, [128, 4, 512], BF16, "kTs")
            kto = [sb(es, [128, 512], BF16, "kto") for _ in range(2)]
            gxo = [sb(es, [128, 512], BF16, "gxo") for _ in range(2)]
            glrT = [sb(es, [17, 512], BF16, "glrT") for _ in range(2)]
            lgt = sb(es, [128, 512], F32, "lgt")
            lgo = [sb(es, [128, 512], F32, "lgo") for _ in range(2)]
            tmo = [[sb(es, [128, 512], BF16, "tmo") for _ in range(2)] for _ in range(4)]
            Gt = [sb(es, [128, 16], F32, "Gt") for _ in range(2)]
            Gtmp = sb(es, [128, 8], F32, "Gtmp")
            pTl = [ps(es, [128, 512], F32, "pT") for _ in range(2)]

            def bfv(t):
                return t[:].bitcast(BF16).rearrange("p (a b) -> p a b", b=128)
            pF = [ps(es, [128, 512], F32, "pF") for _ in range(2)]
            pCVl = [ps(es, [128, 512], F32, "pCV") for _ in range(2)]
            pTM = [ps(es, [128, 512], F32, "pTM") for _ in range(2)]
            for d in range(2):
                S.op("pool", lambda h: h.memset(glrT[d][:], 1.0), W=[glrT[d]])

            cnt = dict(x=0, hT=0, qk=0, kto=0, gxo=0, lgo=0, tm=0, G=0, ev=0, pf=0, ptm=0, pt=0, pcv=0, ab=0)
            allb = [pF[0], pCVl[0], pTM[0], pTl[0], pF[1], pCVl[1], pTM[1], pTl[1]]

            def evac_copy(out, in_, R, W, scale=None):
                if scale is None:
                    S.op("dve", lambda h: h.tensor_copy(out=out, in_=in_), R=R, W=W)
                else:
                    S.op("dve", lambda h: h.tensor_scalar(out=out, in0=in_, scalar1=scale, scalar2=None,
                                                          op0=ALU.mult), R=R, W=W)

            blocks = []
            for j in range(16):
                blocks.append(dict(src="x", t0=512 * j, nt=512, halo=64, full=(j < 8), need_f=(j < 8), r=0))
            blocks.append(dict(src="ctx", t0=SEQ, nt=CTX, halo=0, full=False, need_f=True, r=1))

            def stage1a(bi):
                blk = blocks[bi]
                t0, nt, halo = blk["t0"], blk["nt"], blk["halo"]
                isx = blk["src"] == "x"
                ntile = (nt + 2 * halo) // 128
                ssl, rsl = ssq[bi % 2], rsq[bi % 2]
                xts = []
                for i in range(ntile):
                    xtl = xt[cnt["x"] % len(xt)]
                    cnt["x"] += 1
                    xts.append(xtl)
                    xnl = xn[bi % 2][i]
                    if isx:
                        lo = t0 - halo + 128 * i
                        hi = lo + 128
                        clo, chi = max(lo, 0), min(hi, SEQ)
                        if clo > lo or chi < hi:
                            S.op("pool", lambda h: h.memset(xtl[:], 0.0), W=[xtl])
                        S.dma(xtl[clo - lo:chi - lo, :], x_in[clo:chi, :], W=[xtl])
                    else:
                        S.dma(xtl[:], ctx_in[128 * i:128 * (i + 1), :], W=[xtl])
                    S.op("act", lambda h: h.activation(out=xnl[:], in_=xtl[:], func=AF.Square,
                                                       accum_out=ssl[:, i:i + 1]), R=[xtl], W=[xnl, ssl])
                rstd_from_ss(ssl, rsl, ntile, 1.0 / D)
                for i in range(ntile):
                    xnl = xn[bi % 2][i]
                    S.op("act", lambda h: h.activation(out=xnl[:], in_=xts[i][:], func=AF.Copy,
                                                       scale=rsl[:, i:i + 1]), R=[xts[i], rsl], W=[xnl])

            def stage1b(bi):
                blk = blocks[bi]
                nt, halo, r = blk["nt"], blk["halo"], blk["r"]
                ntile = (nt + 2 * halo) // 128
                h_ = hT[bi % 2]
                def tile_item(i):
                    xnl = xn[bi % 2][i]
                    pT = allb[cnt["ab"] % 8]
                    cnt["ab"] += 1
                    for k in range(8):
                        S.op("pe", lambda h: h.transpose(out=bfv(pT)[:, k, :], in_=xnl[:, k * 128:(k + 1) * 128],
                                                         identity=identb[:]), R=[xnl, identb], W=[pT])
                    for k in range(8):
                        o = h_[:, k, i * 128:(i + 1) * 128]
                        if k != 7:
                            S.op("dve", lambda h: h.tensor_scalar(out=o, in0=bfv(pT)[:, k, :], scalar1=A1[:, k, r:r + 1],
                                                                  scalar2=B1(r, k), op0=ALU.mult, op1=ALU.add),
                                 R=[pT, A1, modc], W=[h_])
                        else:
                            S.op("act", lambda h: h.activation(out=o, in_=bfv(pT)[:, k, :], func=AF.Identity,
                                                               scale=A1[:, k, r:r + 1], bias=B1(r, k)),
                                 R=[pT, A1, modc], W=[h_])
                return [(lambda i=i: tile_item(i)) for i in range(ntile)]

            def interleave(a, b):
                a, b = list(a), list(b)
                n = max(len(a), len(b))
                for i in range(n):
                    if i < len(a):
                        a[i]()
                    lo = (i * len(b)) // n if n else 0
                    hi = ((i + 1) * len(b)) // n if n else 0
                    for j in range(lo, hi):
                        b[j]()

            stage1a(0)
            for it in stage1b(0):
                it()
            for bi, blk in enumerate(blocks):
                t0, nt, halo, full, r = blk["t0"], blk["nt"], blk["halo"], blk["full"], blk["r"]
                isx = blk["src"] == "x"
                W_ = nt + 2 * halo
                ntile = W_ // 128
                h_ = hT[bi % 2]
                if bi + 1 < len(blocks):
                    stage1a(bi + 1)
                c0 = halo
                mch = list(range(8)) if full else list(range(4, 8))
                nhalf = 2 if isx else 1
                hw = W_ // nhalf
                for ch in mch:
                    for hf in range(nhalf):
                        p = allb[cnt["ab"] % 8]
                        cnt["ab"] += 1
                        for k in range(8):
                            S.op("pe", lambda h: h.matmul(p[:, 0:hw], lhsT=wfm[:, k, ch * 128:(ch + 1) * 128],
                                                          rhs=h_[:, k, hf * hw:(hf + 1) * hw],
                                                          start=(k == 0), stop=(k == 7)), R=[wfm, h_], W=[p])
                        evac_copy(zT[:, ch, hf * hw:(hf + 1) * hw], p[:, 0:hw], [p], [zT])
                if isx and t0 == 0:
                    S.op("pool", lambda h: h.memset(zT[:, :, 0:64], 0.0), W=[zT])
                if isx and t0 + nt == SEQ:
                    S.op("pool", lambda h: h.memset(zT[:, :, 576:640], 0.0), W=[zT])
                if isx:
                    rows, cols, taps = 8, 64, [(ky, kx) for ky in range(3) for kx in (1, 0, 2)]
                else:
                    rows, cols, taps = 1, 256, [(1, 1), (1, 0), (1, 2)]
                def conv_item(ch):
                    pCV = allb[cnt["ab"] % 8]
                    cnt["ab"] += 1
                    zv = zT[:, ch, 0:W_].rearrange("p (a b) -> p a b", b=cols)
                    cvv = pCV[:, 0:nt].rearrange("p (a b) -> p a b", b=cols)
                    for ti, (ky, kx) in enumerate(taps):
                        dy, dx = ky - 1, kx - 1
                        r0 = (1 + dy) if isx else 0
                        if dx == 0:
                            o_ap, i_ap = cvv[:, :, :], zv[:, r0:r0 + rows, :]
                        elif dx == -1:
                            o_ap, i_ap = cvv[:, :, 1:cols], zv[:, r0:r0 + rows, 0:cols - 1]
                        else:
                            o_ap, i_ap = cvv[:, :, 0:cols - 1], zv[:, r0:r0 + rows, 1:cols]
                        S.op("pe", lambda h: h.matmul(o_ap, lhsT=dgw[:, (ky * 3 + kx) * 8 + ch, :], rhs=i_ap,
                                                      start=(ti == 0), stop=(ti == len(taps) - 1)),
                             R=[dgw, zT], W=[pCV])
                    if ch < 4:
                        o = qk[cnt["qk"] % 3]
                        cnt["qk"] += 1
                        S.op("act", lambda h: h.activation(out=o[:, 0:nt], in_=pCV[:, 0:nt], func=AF.Silu),
                             R=[pCV], W=[o])
                        S.dma(qT_d[ch * 128:(ch + 1) * 128, t0:t0 + nt], o[:, 0:nt], R=[o], W=[qT_d], owner=o, q="pool")
                    else:
                        S.op("act", lambda h: h.activation(out=kTs[:, ch - 4, 0:nt], in_=pCV[:, 0:nt], func=AF.Silu),
                             R=[pCV], W=[kTs])
                        if full:
                            S.dma(kT_d[(ch - 4) * 128:(ch - 3) * 128, t0:t0 + nt], kTs[:, ch - 4, 0:nt],
                                  R=[kTs], W=[kT_d], owner=kTs, q="pool")
                conv_items = [(lambda ch=ch: conv_item(ch)) for ch in mch]
                interleave(conv_items, stage1b(bi + 1) if bi + 1 < len(blocks) else [])
                def ktr_item(ti):
                    pT2 = allb[cnt["ab"] % 8]
                    cnt["ab"] += 1
                    for hh in range(4):
                        S.op("pe", lambda h: h.transpose(out=bfv(pT2)[:, hh, :], in_=kTs[:, hh, ti * 128:(ti + 1) * 128],
                                                         identity=identb[:]), R=[kTs, identb], W=[pT2])
                    o = kto[cnt["kto"] % 2]
                    cnt["kto"] += 1
                    evac_copy(o[:].rearrange("p (a b) -> p a b", b=128), bfv(pT2)[:, 0:4, :], [pT2], [o])
                    S.dma(ktok_d[t0 + ti * 128:t0 + (ti + 1) * 128, :], o[:], R=[o], W=[ktok_d], owner=o, q="pool")
                gch = [0, 1, 2, 3] if full else [2, 3]
                def g_item(gc):
                    p = allb[cnt["ab"] % 8]
                    cnt["ab"] += 1
                    for k in range(8):
                        S.op("pe", lambda h: h.matmul(p[:, 0:nt], lhsT=wfm[:, k, 1024 + gc * 128:1024 + (gc + 1) * 128],
                                                      rhs=h_[:, k, c0:c0 + nt], start=(k == 0), stop=(k == 7)),
                             R=[wfm, h_], W=[p])
                    o = gxo[cnt["gxo"] % 2]
                    cnt["gxo"] += 1
                    if gc < 2:
                        evac_copy(o[:, 0:nt], p[:, 0:nt], [p], [o], scale=0.125)
                        S.dma(gqT_d[gc * 128:(gc + 1) * 128, t0:t0 + nt], o[:, 0:nt], R=[o], W=[gqT_d], owner=o, q="pool")
                    else:
                        evac_copy(o[:, 0:nt], p[:, 0:nt], [p], [o])
                        S.dma(gkT_d[(gc - 2) * 128:(gc - 1) * 128, t0:t0 + nt], o[:, 0:nt], R=[o], W=[gkT_d], owner=o, q="pool")
                dirs = [0, 1] if blk["need_f"] else [1]
                for d in dirs:
                    pMS = allb[cnt["ab"] % 8]
                    cnt["ab"] += 1
                    for k in range(8):
                        S.op("pe", lambda h: h.matmul(pMS[0:16, 0:nt], lhsT=wfm[:, k, 1536 + 16 * d:1552 + 16 * d],
                                                      rhs=h_[:, k, c0:c0 + nt], start=(k == 0), stop=(k == 7)),
                             R=[wfm, h_], W=[pMS])
                    evac_copy(glrT[d][0:16, 0:nt], pMS[0:16, 0:nt], [pMS], [glrT[d]])
                for gc in gch:
                    g_item(gc)
                def glr_b_item(ti):
                    p = allb[cnt["ab"] % 8]
                    cnt["ab"] += 1
                    for d in dirs:
                        S.op("pe", lambda h: h.matmul(p[:, d * 256:(d + 1) * 256], lhsT=glrT[d][0:17, ti * 128:(ti + 1) * 128],
                                                      rhs=ggw[0:17, d, :], start=True, stop=True),
                             R=[glrT[d], ggw], W=[p])
                    c_lo = dirs[0] * 256
                    S.op("act", lambda h: h.activation(out=lgt[:, c_lo:512], in_=p[:, c_lo:512], func=AF.Exp, scale=-1.0),
                         R=[p], W=[lgt])
                    S.op("act", lambda h: h.activation(out=lgt[:, c_lo:512], in_=lgt[:, c_lo:512], func=AF.Ln, bias=1.0),
                         R=[lgt], W=[lgt])
                    o = lgo[cnt["lgo"] % 2]
                    cnt["lgo"] += 1
                    S.op("dve", lambda h: h.tensor_scalar(out=o[:, c_lo:512], in0=lgt[:, c_lo:512], scalar1=-1.0 / 16.0,
                                                          scalar2=None, op0=ALU.mult), R=[lgt], W=[o])
                    tr = slice(t0 + ti * 128, t0 + (ti + 1) * 128)
                    if blk["need_f"]:
                        S.dma(lgf_d[tr, :], o[:, 0:256], R=[o], W=[lgf_d], owner=o, q="pool")
                    S.dma(lgb_d[tr, :], o[:, 256:512], R=[o], W=[lgb_d], owner=o, q="pool")
                groups = [0, 1, 2, 3, 4] if full else [0, 2, 4]
                def tm_item(ti):
                    tr = slice(t0 + ti * 128, t0 + (ti + 1) * 128)
                    lh = lambda k: h_[:, k, c0 + ti * 128:c0 + (ti + 1) * 128]
                    for g in groups:
                        ncol = 512 if g < 4 else 16
                        p = allb[cnt["ab"] % 8]
                        cnt["ab"] += 1
                        for k in range(8):
                            S.op("pe", lambda h: h.matmul(p[:, 0:ncol], lhsT=lh(k), rhs=wtm[:, k, g * 512:g * 512 + ncol],
                                                          start=(k == 0), stop=(k == 7)), R=[h_, wtm], W=[p])
                        if g < 4:
                            o = tmo[g][cnt["tm"] % 2]
                            dst = (v_d, mo_d, gv_d, gr_d)[g]
                            if g == 0 or g == 2:
                                evac_copy(o[:], p[:], [p], [o])
                            elif g == 1:
                                S.op("act", lambda h: h.activation(out=o[:], in_=p[:], func=AF.Sigmoid), R=[p], W=[o])
                            else:
                                S.op("act", lambda h: h.activation(out=o[:], in_=p[:], func=AF.Silu), R=[p], W=[o])
                            S.dma(dst[tr, :], o[:], R=[o], W=[dst], owner=o, q="pool")
                        else:
                            gt = Gt[cnt["G"] % 2]
                            cnt["G"] += 1
                            S.op("dve", lambda h: h.tensor_tensor(out=gt[:], in0=p[:, 0:16], in1=mgb[:], op=ALU.add),
                                 R=[p, mgb], W=[gt])
                            g3 = gt[:].rearrange("p (a b) -> p a b", b=8)[:, :, 4:8]
                            t3 = Gtmp[:].rearrange("p (a b) -> p a b", b=4)
                            S.op("act", lambda h: h.activation(out=t3, in_=g3, func=AF.Exp, scale=-1.0), R=[gt], W=[Gtmp])
                            S.op("act", lambda h: h.activation(out=t3, in_=t3, func=AF.Ln, bias=1.0), R=[Gtmp], W=[Gtmp])
                            S.op("dve", lambda h: h.tensor_scalar(out=g3, in0=t3, scalar1=-1.0, scalar2=None, op0=ALU.mult),
                                 R=[Gtmp], W=[gt])
                            S.dma(G_d[:, (t0 + ti * 128) // 128, :], gt[:], R=[gt], W=[G_d], owner=gt, q="pool")
                    cnt["tm"] += 1
                late = []
                for ti in range(nt // 128):
                    late.append(lambda ti=ti: glr_b_item(ti))
                    late.append(lambda ti=ti: ktr_item(ti))
                interleave([(lambda ti=ti: tm_item(ti)) for ti in range(nt // 128)], late)
          S.phase_end()

        if STOP_AFTER >= 3:
          S.phase_begin()
          with contextlib.ExitStack() as es:
            G = sb(es, [128, NCH, 16], F32, "G")
            S.dma(G[:], G_d[:, :, :], R=[G_d], W=[G])
            woutb = sb(es, [128, 8, D], BF16, "woutb")
            g1row = sb(es, [128, D], F32, "g1row")
            dg = sb(es, [128, 128], F32, "dg")
            wstg = [sb(es, [128, D], F32, "wstg") for _ in range(2)]
            PSB = {d_: [ps(es, [128, 512], F32, "psb%d" % d_) for _ in range(4)] for d_ in (1, 0)}
            psi = {}

            def nb(d_, grp=0):
                key = (d_, grp)
                n = psi.get(key, 0)
                psi[key] = n + 1
                return PSB[d_][2 * grp + n % 2]

            def v3(t, n=128):
                return t[:].rearrange("p (a b) -> p a b", b=n)

            def vb3(t):
                return t[:].bitcast(BF16).rearrange("p (a b) -> p a b", b=128)

            gb = [PSB[1][0], PSB[1][1]]
            for k in range(8):
                S.op("dve", lambda h: h.tensor_scalar(out=dg[:], in0=identf[:], scalar1=modc[:, 16 + k, 0:1], scalar2=None,
                                                      op0=ALU.mult), R=[identf, modc], W=[dg])
                S.op("pe", lambda h: h.matmul(gb[k // 4][:, (k % 4) * 128:(k % 4 + 1) * 128], lhsT=onesf[:], rhs=dg[:],
                                              start=True, stop=True), R=[onesf, dg], W=[gb[k // 4]])
            for hf in range(2):
                S.op("act", lambda h: h.activation(out=g1row[:, hf * 512:(hf + 1) * 512], in_=gb[hf][:], func=AF.Copy),
                     R=[gb[hf]], W=[g1row])
            for k in range(8):
                st = wstg[k % 2]
                S.dma(st[:], wout_in[k * 128:(k + 1) * 128, :], W=[st])
                S.op("dve", lambda h: h.scalar_tensor_tensor(out=woutb[:, k, :], in0=st[:], scalar=gncol[:, k:k + 1],
                                                             in1=g1row[:], op0=ALU.mult, op1=ALU.mult),
                     R=[st, gncol, g1row], W=[woutb])

            NG = NCH * 4
            lfc = sb(es, [128, NG], F32, "lfc")
            lic = sb(es, [128, NG], F32, "lic")
            ebc = [sb(es, [128, NG], F32, "ebc") for _ in range(2)]
            Ecol = [sb(es, [128, NG], F32, "Ecol") for _ in range(2)]
            etot = [sb(es, [128, NG], F32, "etot") for _ in range(2)]
            for d in range(2):
                PPg, AAg = PSB[0][0], PSB[0][1]
                S.op("dve", lambda h: h.tensor_copy(out=lfc[:].rearrange("p (c g) -> p c g", g=4), in_=G[:, :, 4 + 8 * d:8 + 8 * d]),
                     R=[G], W=[lfc])
                S.op("dve", lambda h: h.tensor_copy(out=lic[:].rearrange("p (c g) -> p c g", g=4), in_=G[:, :, 8 * d:8 * d + 4]),
                     R=[G], W=[lic])
                S.op("pe", lambda h: h.matmul(PPg[:, 0:NG], lhsT=maskf[d][:], rhs=lfc[:], start=True, stop=True),
                     R=[maskf[d], lfc], W=[PPg])
                S.op("pe", lambda h: h.matmul(AAg[:, 0:NG], lhsT=onesf[:], rhs=lfc[:], start=True, stop=True),
                     R=[onesf, lfc], W=[AAg])
                S.op("act", lambda h: h.activation(out=ebc[d][:], in_=PPg[:, 0:NG], func=AF.Exp), R=[PPg], W=[ebc[d]])
                S.op("dve", lambda h: h.tensor_tensor(out=lic[:], in0=lic[:], in1=PPg[:, 0:NG], op=ALU.subtract),
                     R=[lic, PPg], W=[lic])
                S.op("act", lambda h: h.activation(out=Ecol[d][:], in_=lic[:], func=AF.Exp, bias=lnms[:, 0:1]), R=[lic, lnms], W=[Ecol[d]])
                S.op("act", lambda h: h.activation(out=etot[d][:], in_=AAg[:, 0:NG], func=AF.Exp), R=[AAg], W=[etot[d]])

            class NS:
                pass
            NB = 2
            rowm = sb(es, [128, 2], F32, "rowm")
            S.op("pool", lambda h: h.memset(rowm[:], 0.0), W=[rowm])
            S.op("pool", lambda h: h.memset(rowm[0:64, 0:1], 1.0), W=[rowm])
            S.op("pool", lambda h: h.memset(rowm[64:128, 1:2], 1.0), W=[rowm])
            epsb = sb(es, [128, 1], F32, "epsb")
            S.op("pool", lambda h: h.memset(epsb[:], EPS), W=[epsb])

            def mk_stream(tag):
                T = NS()
                T.qTt = [sb(es, [128, 4, 128], BF16, "qTt" + tag) for _ in range(NB)]
                T.kTt = [sb(es, [128, 4, 128], BF16, "kTt" + tag) for _ in range(NB)]
                T.ktk = [sb(es, [128, 512], BF16, "ktk" + tag) for _ in range(NB)]
                T.vx = [sb(es, [128, 4, 129], BF16, "vx" + tag) for _ in range(NB)]
                T.gqt = [sb(es, [128, 2, 128], BF16, "gqt" + tag) for _ in range(NB)]
                T.gkt = [sb(es, [128, 2, 128], BF16, "gkt" + tag) for _ in range(NB)]
                T.gvt = [sb(es, [128, 512], BF16, "gvt" + tag) for _ in range(NB)]
                T.lgl = [sb(es, [128, 256], F32, "lgl" + tag) for _ in range(NB)]
                T.OTt = [sb(es, [128, D], F32, "OTt" + tag) for _ in range(NB)]
                T.vd = sb(es, [128, 4, 129], BF16, "vd" + tag)
                T.Pm = sb(es, [128, 4, 128], BF16, "Pm" + tag)
                T.Hs = sb(es, [128, 4, 129], F32, "Hs" + tag)
                T.dn = sb(es, [128, 4], F32, "dn" + tag)
                T.dn2 = sb(es, [128, 4], F32, "dn2" + tag)
                T.C32 = sb(es, [128, 4, 129], F32, "C32" + tag)
                T.Cbf = sb(es, [128, 4, 129], BF16, "Cbf" + tag)
                T.S32 = sb(es, [128, 2, 128], F32, "S32" + tag)
                T.ekt = sb(es, [128, 2, 128], F32, "ekt" + tag)
                T.eqt = sb(es, [128, 2, 128], F32, "eqt" + tag)
                T.etg = sb(es, [128, 2], F32, "etg" + tag)
                T.kd = sb(es, [128, 2, 128], BF16, "kd" + tag)
                T.qd = sb(es, [128, 2, 128], BF16, "qd" + tag)
                T.kdt = sb(es, [128, 2, 128], BF16, "kdt" + tag)
                T.Am = sb(es, [128, 4, 128], BF16, "Am" + tag)
                T.kdz = [sb(es, [128, 2, 128], BF16, "kdz" + tag) for _ in range(2)]
                T.Sz = [sb(es, [128, 2, 128], BF16, "Sz" + tag) for _ in range(2)]
                T.etgm = sb(es, [128, 2, 2], F32, "etgm" + tag)
                T.HB = sb(es, [128, D], F32, "HB" + tag)
                T.GT = sb(es, [128, D], BF16, "GT" + tag)
                T.XT = sb(es, [128, D], F32, "XT" + tag)
                T.X1 = sb(es, [128, D], F32, "X1" + tag)
                T.SQ = sb(es, [128, D], F32, "SQ" + tag)
                T.ss8 = sb(es, [128, 8], F32, "ss8" + tag)
                T.rs8 = sb(es, [128, 8], F32, "rs8" + tag)
                T.MX = sb(es, [128, D], BF16, "MX" + tag)
                T.mixT = sb(es, [128, 8, 128], BF16, "mixT" + tag)
                for i in range(NB):
                    S.op("pool", lambda h: h.memset(T.vx[i][:], 1.0), W=[T.vx[i]])
                S.op("pool", lambda h: h.memset(T.C32[:], 0.0), W=[T.C32])
                S.op("pool", lambda h: h.memset(T.Cbf[:], 0.0), W=[T.Cbf])
                S.op("pool", lambda h: h.memset(T.S32[:], 0.0), W=[T.S32])
                for j in range(2):
                    S.op("pool", lambda h: h.memset(T.Sz[j][:], 0.0), W=[T.Sz[j]])
                return T

            ST = {1: mk_stream("b"), 0: mk_stream("f")}

            def load_step(slot, d, ci, with_out, ts=None):
                T = ST[d if ts is None else ts]
                tr = slice(ci * 128, (ci + 1) * 128)
                if with_out:
                    S.dma(T.qTt[slot][:], qT_d[:, tr].rearrange("(h p) t -> p h t", p=128), R=[qT_d], W=[T.qTt[slot]])
                    S.dma(T.kTt[slot][:], kT_d[:, tr].rearrange("(h p) t -> p h t", p=128), R=[kT_d], W=[T.kTt[slot]])
                    S.dma(T.gqt[slot][:], gqT_d[:, tr].rearrange("(g p) t -> p g t", p=128), R=[gqT_d], W=[T.gqt[slot]])
                S.dma(T.ktk[slot][:], ktok_d[tr, :], R=[ktok_d], W=[T.ktk[slot]])
                S.dma(T.vx[slot][:, :, 0:128], v_d[tr, :].rearrange("t (h e) -> t h e", e=128), R=[v_d], W=[T.vx[slot]])
                S.dma(T.gkt[slot][:], gkT_d[:, tr].rearrange("(g p) t -> p g t", p=128), R=[gkT_d], W=[T.gkt[slot]])
                S.dma(T.gvt[slot][:], gv_d[tr, :], R=[gv_d], W=[T.gvt[slot]])
                lsrc = lgf_d if d == 0 else lgb_d
                S.dma(T.lgl[slot][:], lsrc[tr, :], R=[lsrc], W=[T.lgl[slot]])

            def load_combine(d, ci):
                T = ST[d]
                tr = slice(ci * 128, (ci + 1) * 128)
                S.dma(T.HB[:], hb_d[tr, :], R=[hb_d], W=[T.HB])
                S.dma(T.GT[:, 0:512], mo_d[tr, :], R=[mo_d], W=[T.GT])
                S.dma(T.GT[:, 512:1024], gr_d[tr, :], R=[gr_d], W=[T.GT])
                S.dma(T.XT[:], x_in[tr, :], W=[T.XT])

            def mlstm_gen(slot, d, ci, with_out, need_bf=True, ts=None):
                ts = d if ts is None else ts
                T = ST[ts]
                vd, Pm, Hs, dn, dn2, C32, Cbf = T.vd, T.Pm, T.Hs, T.dn, T.dn2, T.C32, T.Cbf
                gs = slice(ci * 4, ci * 4 + 4)
                q_, k_, kt_, vx_ = T.qTt[slot], T.kTt[slot], T.ktk[slot], T.vx[slot]
                OT = T.OTt[slot]
                mk3 = maskf[d][:].unsqueeze(1).broadcast_to([128, 4, 128])
                S.op("dve", lambda h: h.tensor_tensor(out=vd[:], in0=vx_[:], in1=Ecol[d][:, gs].unsqueeze(2).broadcast_to([128, 4, 129]),
                                                       op=ALU.mult), R=[vx_, Ecol[d]], W=[vd])
                yield
                if with_out:
                    PPt = nb(ts)
                    for hh in range(4):
                        S.op("pe", lambda h: h.matmul(v3(PPt)[:, hh, :], lhsT=k_[:, hh, :], rhs=q_[:, hh, :], start=True, stop=True),
                             R=[k_, q_], W=[PPt])
                    yield
                    S.op("dve", lambda h: h.tensor_tensor(out=Pm[:], in0=v3(PPt), in1=mk3, op=ALU.mult),
                         R=[PPt, maskf[d]], W=[Pm])
                    yield
                    Hb_ = [nb(ts), nb(ts)]
                    for hh in range(4):
                        o = Hb_[hh // 2][:, (hh % 2) * 129:(hh % 2 + 1) * 129]
                        S.op("pe", lambda h: h.matmul(o, lhsT=q_[:, hh, :], rhs=Cbf[:, hh, :], start=True, stop=False),
                             R=[q_, Cbf], W=[Hb_[hh // 2]])
                        S.op("pe", lambda h: h.matmul(o, lhsT=Pm[:, hh, :], rhs=vd[:, hh, :], start=False, stop=True),
                             R=[Pm, vd], W=[Hb_[hh // 2]])
                        if hh == 1:
                            yield
                    yield
                    for a in range(2):
                        eb3 = ebc[d][:, ci * 4 + 2 * a:ci * 4 + 2 * a + 2].unsqueeze(2).broadcast_to([128, 2, 129])
                        S.op("dve", lambda h: h.tensor_tensor(out=Hs[:, 2 * a:2 * a + 2, :], in0=Hb_[a][:, 0:258].rearrange("p (b c) -> p b c", c=129),
                                                              in1=eb3, op=ALU.mult), R=[Hb_[a], ebc[d]], W=[Hs])
                        yield
                    S.op("dve", lambda h: h.tensor_scalar(out=dn2[:], in0=Hs[:, :, 128], scalar1=-1.0, scalar2=1.0, op0=ALU.mult, op1=ALU.max),
                         R=[Hs], W=[dn2])
                    yield
                    S.op("dve", lambda h: h.tensor_tensor(out=dn[:], in0=dn2[:], in1=Hs[:, :, 128], op=ALU.max), R=[dn2, Hs], W=[dn])
                    yield
                    S.op("dve", lambda h: h.reciprocal(out=dn[:], in_=dn[:]), R=[dn], W=[dn])
                    yield
                    for hh in range(4):
                        S.op("act", lambda h: h.activation(out=OT[:, hh * 128:(hh + 1) * 128], in_=Hs[:, hh, 0:128], func=AF.Copy,
                                                           scale=dn[:, hh:hh + 1]), R=[Hs, dn], W=[OT])
                    yield
                Ub_ = [nb(ts), nb(ts)]
                for hh in range(4):
                    o = Ub_[hh // 2][:, (hh % 2) * 129:(hh % 2 + 1) * 129]
                    S.op("pe", lambda h: h.matmul(o, lhsT=kt_[:, hh * 128:(hh + 1) * 128], rhs=vd[:, hh, :],
                                                  start=True, stop=True), R=[kt_, vd], W=[Ub_[hh // 2]])
                yield
                for a in range(2):
                    S.op("dve", lambda h: h.tensor_tensor(out=C32[:, 2 * a:2 * a + 2, :], in0=C32[:, 2 * a:2 * a + 2, :],
                                                          in1=Ub_[a][:, 0:258].rearrange("p (b c) -> p b c", c=129), op=ALU.add),
                         R=[C32, Ub_[a]], W=[C32])
                    yield
                et3 = etot[d][:, gs].unsqueeze(2).broadcast_to([128, 4, 129])
                S.op("dve", lambda h: h.tensor_tensor(out=C32[:], in0=C32[:], in1=et3, op=ALU.mult), R=[C32, etot[d]], W=[C32])
                yield
                if need_bf:
                    S.op("act", lambda h: h.activation(out=Cbf[:], in_=C32[:], func=AF.Copy), R=[C32], W=[Cbf])
                    yield

            def gla_gen(slot, d, ci, with_out, need_bf=True, ts=None, acc=None):
                ts = d if ts is None else ts
                T = ST[ts]
                S32, ekt, eqt, etg, kd, qd, kdt, Am, kdz, Sz, etgm = (T.S32, T.ekt, T.eqt, T.etg, T.kd, T.qd, T.kdt, T.Am,
                                                                     T.kdz, T.Sz, T.etgm)
                gq_, gk_, gv_, lg_ = T.gqt[slot], T.gkt[slot], T.gvt[slot], T.lgl[slot]
                OT = T.OTt[slot]
                mk3 = maskf[d][:].unsqueeze(1).broadcast_to([128, 4, 128])
                CUMt = nb(ts, 1)
                C3 = v3(CUMt)
                for g in range(2):
                    S.op("pe", lambda h: h.matmul(C3[:, g, :], lhsT=lg_[:, g * 128:(g + 1) * 128], rhs=maskf[d][:],
                                                  start=True, stop=True), R=[lg_, maskf[d]], W=[CUMt])
                yield
                S.op("act", lambda h: h.activation(out=ekt[:], in_=C3[:, 0:2, :], func=AF.Exp, scale=-1.0), R=[CUMt], W=[ekt])
                tcol = 127 if d == 0 else 0
                S.op("act", lambda h: h.activation(out=etg[:], in_=C3[:, 0:2, tcol], func=AF.Exp), R=[CUMt], W=[etg])
                if acc is not None:
                    S.op("dve", lambda h: h.tensor_tensor(out=acc[:], in0=acc[:], in1=etg[:], op=ALU.mult), R=[acc, etg], W=[acc])
                if with_out:
                    S.op("act", lambda h: h.activation(out=eqt[:], in_=C3[:, 0:2, :], func=AF.Exp), R=[CUMt], W=[eqt])
                yield
                S.op("pool", lambda h: h.tensor_tensor(out=kd[:], in0=gk_[:], in1=ekt[:], op=ALU.mult), R=[gk_, ekt], W=[kd])
                yield
                if with_out:
                    S.op("pool", lambda h: h.tensor_tensor(out=qd[:], in0=gq_[:], in1=eqt[:], op=ALU.mult), R=[gq_, eqt], W=[qd])
                    for j in range(2):
                        S.op("dve", lambda h: h.tensor_scalar(out=kdz[j][:], in0=kd[:], scalar1=rowm[:, j:j + 1], scalar2=None,
                                                              op0=ALU.mult), R=[kd, rowm], W=[kdz[j]])
                    yield
                    AAt = nb(ts, 1)
                    for hh in range(4):
                        g, j = hh // 2, hh % 2
                        S.op("pe", lambda h: h.matmul(v3(AAt)[:, hh, :], lhsT=kdz[j][:, g, :], rhs=qd[:, g, :],
                                                      start=True, stop=True), R=[kdz[j], qd], W=[AAt])
                    yield
                    S.op("dve", lambda h: h.tensor_tensor(out=Am[:], in0=v3(AAt), in1=mk3, op=ALU.mult),
                         R=[AAt, maskf[d]], W=[Am])
                    yield
                    OOt = nb(ts, 1)
                    for hh in range(4):
                        g, j = hh // 2, hh % 2
                        S.op("pe", lambda h: h.matmul(v3(OOt)[:, hh, :], lhsT=qd[:, g, :], rhs=Sz[j][:, g, :],
                                                      start=True, stop=False), R=[qd, Sz[j]], W=[OOt])
                        S.op("pe", lambda h: h.matmul(v3(OOt)[:, hh, :], lhsT=Am[:, hh, :], rhs=gv_[:, hh * 128:(hh + 1) * 128],
                                                      start=False, stop=True), R=[Am, gv_], W=[OOt])
                        if hh == 1:
                            yield
                    yield
                    S.op("act", lambda h: h.activation(out=OT[:, 512:1024], in_=OOt[:], func=AF.Copy), R=[OOt], W=[OT])
                    yield
                Tt = nb(ts, 1)
                for g in range(2):
                    S.op("pe", lambda h: h.transpose(out=vb3(Tt)[:, g, :], in_=kd[:, g, :], identity=identb[:]), R=[kd, identb], W=[Tt])
                yield
                S.op("act", lambda h: h.activation(out=kdt[:], in_=vb3(Tt)[:, 0:2, :], func=AF.Copy), R=[Tt], W=[kdt])
                yield
                UGt = nb(ts, 1)
                for g in range(2):
                    S.op("pe", lambda h: h.matmul(UGt[:, g * 256:(g + 1) * 256], lhsT=kdt[:, g, :], rhs=gv_[:, g * 256:(g + 1) * 256],
                                                  start=True, stop=True), R=[kdt, gv_], W=[UGt])
                yield
                for g in range(2):
                    for j in range(2):
                        ps_ = slice(j * 64, (j + 1) * 64)
                        S.op("dve", lambda h: h.tensor_tensor(out=S32[ps_, g, :], in0=S32[ps_, g, :],
                                                              in1=UGt[ps_, g * 256 + j * 128:g * 256 + (j + 1) * 128], op=ALU.add),
                             R=[S32, UGt], W=[S32])
                    yield
                for g in range(2):
                    S.op("act", lambda h: h.activation(out=S32[:, g, :], in_=S32[:, g, :], func=AF.Copy, scale=etg[:, g:g + 1]),
                         R=[S32, etg], W=[S32])
                yield
                if need_bf:
                    for j in range(2):
                        S.op("act", lambda h: h.activation(out=Sz[j][:], in_=S32[:], func=AF.Copy, scale=rowm[:, j:j + 1]),
                             R=[S32, rowm], W=[Sz[j]])
                    yield

            def out_gen(slot, d, ci, with_out, combine):
                T = ST[d]
                tr = slice(ci * 128, (ci + 1) * 128)
                OT = T.OTt[slot]
                if not with_out:
                    return
                if not combine:
                    S.dma(hb_d[tr, :], OT[:], R=[OT], W=[hb_d], owner=OT)
                    yield
                    return
                HB, GT, XT, X1, SQ, ss8, rs8, MX, mixT = T.HB, T.GT, T.XT, T.X1, T.SQ, T.ss8, T.rs8, T.MX, T.mixT
                S.op("pool", lambda h: h.tensor_tensor(out=HB[:], in0=HB[:], in1=OT[:], op=ALU.add), R=[HB, OT], W=[HB])
                yield
                S.op("dve", lambda h: h.tensor_tensor(out=SQ[:], in0=HB[:], in1=HB[:], op=ALU.mult), R=[HB], W=[SQ])
                yield
                S.op("dve", lambda h: h.tensor_reduce(out=ss8[:], in_=SQ[:].rearrange("p (a b) -> p a b", b=128), axis=AX.X,
                                                      op=ALU.add), R=[SQ], W=[ss8])
                yield
                S.op("act", lambda h: h.activation(out=rs8[:], in_=ss8[:], func=AF.Ln, scale=1.0 / 128.0, bias=epsb[:, 0:1]),
                     R=[ss8, epsb], W=[rs8])
                yield
                S.op("act", lambda h: h.activation(out=rs8[:], in_=rs8[:], func=AF.Exp, scale=-0.5), R=[rs8], W=[rs8])
                yield
                S.op("dve", lambda h: h.tensor_tensor(out=SQ[:].rearrange("p (a b) -> p a b", b=128),
                                                      in0=HB[:].rearrange("p (a b) -> p a b", b=128),
                                                      in1=rs8[:].unsqueeze(2).broadcast_to([128, 8, 128]), op=ALU.mult),
                     R=[HB, rs8], W=[SQ])
                yield
                S.op("pool", lambda h: h.tensor_tensor(out=MX[:], in0=SQ[:], in1=GT[:], op=ALU.mult), R=[SQ, GT], W=[MX])
                yield
                TPt = nb(d, 1)
                for k in range(8):
                    S.op("pe", lambda h: h.transpose(out=vb3(TPt)[:, k, :], in_=MX[:, k * 128:(k + 1) * 128], identity=identb[:]),
                         R=[MX, identb], W=[TPt])
                    if k == 3:
                        yield
                yield
                S.op("act", lambda h: h.activation(out=mixT[:], in_=vb3(TPt), func=AF.Copy), R=[TPt], W=[mixT])
                yield
                for hf in range(2):
                    POt = nb(d, 1)
                    for k in range(8):
                        S.op("pe", lambda h: h.matmul(POt[:], lhsT=mixT[:, k, :], rhs=woutb[:, k, hf * 512:(hf + 1) * 512],
                                                      start=(k == 0), stop=(k == 7)), R=[mixT, woutb], W=[POt])
                        if k == 3:
                            yield
                    yield
                    S.op("dve", lambda h: h.tensor_tensor(out=X1[:, hf * 512:(hf + 1) * 512], in0=XT[:, hf * 512:(hf + 1) * 512],
                                                          in1=POt[:], op=ALU.add), R=[XT, POt], W=[X1])
                    yield
                S.dma(x1_d[tr, :], X1[:], R=[X1], W=[x1_d], owner=X1)
                yield

            def step_gen(slot, d, ci, with_out, combine):
                yield from mlstm_gen(slot, d, ci, with_out)
                yield from gla_gen(slot, d, ci, with_out)
                yield from out_gen(slot, d, ci, with_out, combine)

            def run(gens):
                alive = list(gens)
                while alive:
                    for g in list(alive):
                        try:
                            next(g)
                        except StopIteration:
                            alive.remove(g)

            A_lead = [(65, False), (64, False)] + [(c, False) for c in range(63, 47, -1)]
            S_lead = [(c, False) for c in range(47, 31, -1)]
            pis = sb(es, [128, 2], F32, "pis")
            pic = sb(es, [128, 4], F32, "pic")
            S.op("pool", lambda h: h.memset(pis[:], 1.0), W=[pis])
            S.op("pool", lambda h: h.memset(pic[:], 1.0), W=[pic])
            for (c, _) in S_lead:
                S.op("dve", lambda h: h.tensor_tensor(out=pic[:], in0=pic[:], in1=etot[1][:, c * 4:c * 4 + 4], op=ALU.mult),
                     R=[pic, etot[1]], W=[pic])
            load_step(0, 1, A_lead[0][0], False, ts=1)
            load_step(0, 1, S_lead[0][0], False, ts=0)
            for i in range(len(A_lead)):
                if i + 1 < len(A_lead):
                    load_step((i + 1) % NB, 1, A_lead[i + 1][0], False, ts=1)
                if i + 1 < len(S_lead):
                    load_step((i + 1) % NB, 1, S_lead[i + 1][0], False, ts=0)
                gens = [mlstm_gen(i % NB, 1, A_lead[i][0], False, False, ts=1), gla_gen(i % NB, 1, A_lead[i][0], False, False, ts=1)]
                if i < len(S_lead):
                    gens += [mlstm_gen(i % NB, 1, S_lead[i][0], False, False, ts=0),
                             gla_gen(i % NB, 1, S_lead[i][0], False, False, ts=0, acc=pis)]
                run(gens)
            TA, TS_ = ST[1], ST[0]
            S.op("dve", lambda h: h.tensor_tensor(out=TA.C32[:], in0=TA.C32[:], in1=pic[:].unsqueeze(2).broadcast_to([128, 4, 129]),
                                                  op=ALU.mult), R=[TA.C32, pic], W=[TA.C32])
            S.op("dve", lambda h: h.tensor_tensor(out=TA.C32[:], in0=TA.C32[:], in1=TS_.C32[:], op=ALU.add),
                 R=[TA.C32, TS_.C32], W=[TA.C32])
            S.op("act", lambda h: h.activation(out=TA.Cbf[:], in_=TA.C32[:], func=AF.Copy), R=[TA.C32], W=[TA.Cbf])
            for g in range(2):
                S.op("dve", lambda h: h.scalar_tensor_tensor(out=TA.S32[:, g, :], in0=TA.S32[:, g, :], scalar=pis[:, g:g + 1],
                                                             in1=TS_.S32[:, g, :], op0=ALU.mult, op1=ALU.add),
                     R=[TA.S32, pis, TS_.S32], W=[TA.S32])
                for j in range(2):
                    S.op("act", lambda h: h.activation(out=TA.Sz[j][:, g, :], in_=TA.S32[:, g, :], func=AF.Copy,
                                                       scale=rowm[:, j:j + 1]), R=[TA.S32, rowm], W=[TA.Sz[j]])
            S.op("pool", lambda h: h.memset(TS_.C32[:], 0.0), W=[TS_.C32])
            S.op("pool", lambda h: h.memset(TS_.S32[:], 0.0), W=[TS_.S32])

            seqB = [(64, False), (65, False)] + [(c, True) for c in range(0, 32)]
            seqA = [None, None] + [(c, True) for c in range(31, -1, -1)]
            nI = len(seqB)
            combA = lambda c: c <= 15
            combB = lambda c: c >= 16

            def chain(*gs):
                for g_ in gs:
                    if g_ is not None:
                        yield from g_

            load_step(0, 0, seqB[0][0], seqB[0][1])
            for i in range(nI + 1):
                a = seqA[i] if i < nI else None
                b = seqB[i] if i < nI else None
                an = seqA[i + 1] if i + 1 < nI else None
                bn = seqB[i + 1] if i + 1 < nI else None
                ap = seqA[i - 1] if i - 1 >= 0 else None
                bp = seqB[i - 1] if i - 1 >= 0 else None
                if an is not None:
                    load_step((i + 1) % NB, 1, an[0], an[1])
                if bn is not None:
                    load_step((i + 1) % NB, 0, bn[0], bn[1])
                gens = []
                oA = out_gen((i - 1) % NB, 1, ap[0], ap[1], combA(ap[0])) if ap is not None else None
                oB = out_gen((i - 1) % NB, 0, bp[0], bp[1], combB(bp[0])) if bp is not None else None
                if a is not None:
                    nbfA = an is not None and an[1]
                    gens += [mlstm_gen(i % NB, 1, a[0], a[1], nbfA), chain(gla_gen(i % NB, 1, a[0], a[1], nbfA), oA)]
                elif oA is not None:
                    gens.append(oA)
                if b is not None:
                    nbfB = bn is not None and bn[1]
                    gens += [mlstm_gen(i % NB, 0, b[0], b[1], nbfB), chain(gla_gen(i % NB, 0, b[0], b[1], nbfB), oB)]
                elif oB is not None:
                    gens.append(oB)
                run(gens)
                if a is not None and a[1] and combA(a[0]):
                    load_combine(1, a[0])
                if b is not None and b[1] and combB(b[0]):
                    load_combine(0, b[0])
          S.phase_end()

        if STOP_AFTER >= 4:
          S.phase_begin()
          with contextlib.ExitStack() as es:
            wgu = sb(es, [128, 8, 2 * DFF], BF16, "wgu")
            wdn = sb(es, [128, 22, D], BF16, "wdn")
            gfin = sb(es, [128, D], F32, "gfin")
            PW = 1408
            TPl = [ps(es, [128, 512], F32, "TPb") for _ in range(2)]

            def bfv4(t):
                return t[:].bitcast(BF16).rearrange("p (a b) -> p a b", b=128)
            PSa = [ps(es, [128, 512], F32, "PSa") for _ in range(2)]
            PSg = [ps(es, [128, 512], F32, "PSg") for _ in range(2)]
            PSdl = [ps(es, [128, 512], F32, "PSd") for _ in range(2)]
            allb4 = [PSa[0], PSg[0], PSdl[0], TPl[0], PSa[1], PSg[1], PSdl[1], TPl[1]]
            cnt4 = [0]

            def nb4():
                t = allb4[cnt4[0] % 8]
                cnt4[0] += 1
                return t
            S.dma(gfin[:], gfin_in.broadcast_to([128, D]), W=[gfin])
            with contextlib.ExitStack() as ses:
                NST = 6
                stg2 = [sb(ses, [128, PW], F32, "stg2") for _ in range(NST)]
                g2row = sb(ses, [128, D], F32, "g2row")
                dg4 = sb(ses, [128, 128], F32, "dg4")
                for k in range(8):
                    S.op("dve", lambda h: h.tensor_scalar(out=dg4[:], in0=identf[:], scalar1=modc[:, 40 + k, 0:1], scalar2=None,
                                                          op0=ALU.mult), R=[identf, modc], W=[dg4])
                    S.op("pe", lambda h: h.matmul(PSdl[k // 4][:, (k % 4) * 128:(k % 4 + 1) * 128], lhsT=onesf[:], rhs=dg4[:], start=True, stop=True),
                         R=[onesf, dg4], W=[PSdl[k // 4]])
                for hf in range(2):
                    S.op("act", lambda h: h.activation(out=g2row[:, hf * 512:(hf + 1) * 512], in_=PSdl[hf][:],
                                                       func=AF.Copy), R=[PSdl[hf]], W=[g2row])
                ci_ = 0
                for fc in range(22):
                    st = stg2[ci_ % NST]
                    ci_ += 1
                    S.dma(st[:, 0:D], wdn_in[fc * 128:(fc + 1) * 128, :], W=[st])
                    S.op("dve", lambda h: h.tensor_tensor(out=wdn[:, fc, :], in0=st[:, 0:D], in1=g2row[:], op=ALU.mult),
                         R=[st, g2row], W=[wdn])
                for k in range(8):
                    for pc in range(4):
                        st = stg2[ci_ % NST]
                        ci_ += 1
                        S.dma(st[:, :], wgu_in[k * 128:(k + 1) * 128, pc * PW:(pc + 1) * PW], W=[st])
                        o_ = wgu[:, k, pc * PW:(pc + 1) * PW]
                        if ci_ % 3 == 0:
                            S.op("dve", lambda h: h.tensor_copy(out=o_, in_=st[:, :]), R=[st], W=[wgu])
                        elif ci_ % 3 == 1:
                            S.op("pool", lambda h: h.tensor_copy(out=o_, in_=st[:, :]), R=[st], W=[wgu])
                        else:
                            S.op("act", lambda h: h.activation(out=o_, in_=st[:, :], func=AF.Copy), R=[st], W=[wgu])
            S.phase_end()
            x1b = sb(es, [128, 2, D], F32, "x1b")
            h2T = sb(es, [128, 8, 256], BF16, "h2T")
            uT = sb(es, [128, 22, 256], BF16, "uT")
            sa = [sb(es, [128, 256], F32, "sa") for _ in range(2)]
            x2 = [sb(es, [128, D], F32, "x2") for _ in range(2)]
            xn4 = [sb(es, [128, D], BF16, "xn4") for _ in range(2)]
            junk4 = sb(es, [128, D], BF16, "junk4")
            ss4 = [sb(es, [128, 2], F32, "ss4") for _ in range(2)]
            rs4 = [sb(es, [128, 2], F32, "rs4") for _ in range(2)]

            nblk = OWN // 256
            xn4s = [[xn4[0], xn4[1]], [sb(es, [128, D], BF16, "xn4c"), sb(es, [128, D], BF16, "xn4d")]]
            h2Ts = [h2T, sb(es, [128, 8, 256], BF16, "h2Tb")]
            ssf = [sb(es, [128, 2], F32, "ssf") for _ in range(2)]
            rsf = [sb(es, [128, 2], F32, "rsf") for _ in range(2)]

            def p4a_pieces(bi):
                ssl, rsl = ss4[bi % 2], rs4[bi % 2]
                def sq(sub):
                    rows = slice(bi * 256 + sub * 128, bi * 256 + (sub + 1) * 128)
                    S.dma(x1b[:, sub, :], x1_d[rows, :], R=[x1_d], W=[x1b])
                    S.op("act", lambda h: h.activation(out=xn4s[bi % 2][sub][:], in_=x1b[:, sub, :], func=AF.Square,
                                                       accum_out=ssl[:, sub:sub + 1]), R=[x1b], W=[xn4s[bi % 2][sub], ssl])
                def cp(sub):
                    S.op("act", lambda h: h.activation(out=xn4s[bi % 2][sub][:], in_=x1b[:, sub, :], func=AF.Copy,
                                                       scale=rsl[:, sub:sub + 1]), R=[x1b, rsl], W=[xn4s[bi % 2][sub]])
                return [lambda: sq(0), lambda: sq(1), lambda: rstd_from_ss(ssl, rsl, 2, 1.0 / D), lambda: cp(0), lambda: cp(1)]

            def p4b(bi):
                hh_ = h2Ts[bi % 2]
                for sub in range(2):
                    xnl = xn4s[bi % 2][sub]
                    TPb = nb4()
                    for k in range(8):
                        S.op("pe", lambda h: h.transpose(out=bfv4(TPb)[:, k, :], in_=xnl[:, k * 128:(k + 1) * 128], identity=identb[:]),
                             R=[xnl, identb], W=[TPb])
                    for k in range(8):
                        o = hh_[:, k, sub * 128:(sub + 1) * 128]
                        if k % 4 != 3:
                            S.op("dve", lambda h: h.tensor_scalar(out=o, in0=bfv4(TPb)[:, k, :], scalar1=A2[:, k:k + 1],
                                                                  scalar2=modc[:, 24 + k, 0:1], op0=ALU.mult, op1=ALU.add),
                                 R=[TPb, A2, modc], W=[hh_])
                        else:
                            S.op("act", lambda h: h.activation(out=o, in_=bfv4(TPb)[:, k, :], func=AF.Identity, scale=A2[:, k:k + 1],
                                                               bias=modc[:, 24 + k, 0:1]), R=[TPb, A2, modc], W=[hh_])

            for pc_ in p4a_pieces(0):
                pc_()
            p4b(0)
            xi = 0
            for bi in range(nblk):
                hcur = h2Ts[bi % 2]
                pieces = p4a_pieces(bi + 1) if bi + 1 < nblk else []
                for fc in range(22):
                    pa, pg, sal = nb4(), nb4(), sa[fc % 2]
                    for k in range(8):
                        S.op("pe", lambda h: h.matmul(pa[:, 0:256], lhsT=wgu[:, k, fc * 128:(fc + 1) * 128], rhs=hcur[:, k, :],
                                                      start=(k == 0), stop=(k == 7)), R=[wgu, hcur], W=[pa])
                    for k in range(8):
                        S.op("pe", lambda h: h.matmul(pg[:, 0:256], lhsT=wgu[:, k, DFF + fc * 128:DFF + (fc + 1) * 128], rhs=hcur[:, k, :],
                                                      start=(k == 0), stop=(k == 7)), R=[wgu, hcur], W=[pg])
                    S.op("act", lambda h: h.activation(out=sal[:], in_=pa[:, 0:256], func=AF.Silu), R=[pa], W=[sal])
                    S.op("dve", lambda h: h.tensor_tensor(out=uT[:, fc, :], in0=sal[:], in1=pg[:, 0:256], op=ALU.mult),
                         R=[sal, pg], W=[uT])
                    if pieces and fc >= 2 and fc % 2 == 0 and (fc - 2) // 2 < len(pieces):
                        pieces[(fc - 2) // 2]()
                if bi + 1 < nblk:
                    p4b(bi + 1)
                for sub in range(2):
                    rows = slice(bi * 256 + sub * 128, bi * 256 + (sub + 1) * 128)
                    x2l = x2[xi % 2]
                    ssl, rsl = ssf[xi % 2], rsf[xi % 2]
                    xi += 1
                    S.dma(x2l[:], x1_d[rows, :], R=[x1_d], W=[x2l])
                    for hf in range(2):
                        pd = nb4()
                        for fc in range(22):
                            S.op("pe", lambda h: h.matmul(pd[:], lhsT=uT[:, fc, sub * 128:(sub + 1) * 128],
                                                          rhs=wdn[:, fc, hf * 512:(hf + 1) * 512], start=(fc == 0), stop=(fc == 21)),
                                 R=[uT, wdn], W=[pd])
                        S.op("dve", lambda h: h.tensor_tensor(out=x2l[:, hf * 512:(hf + 1) * 512], in0=x2l[:, hf * 512:(hf + 1) * 512],
                                                              in1=pd[:], op=ALU.add), R=[x2l, pd], W=[x2l])
                    S.op("act", lambda h: h.activation(out=junk4[:], in_=x2l[:], func=AF.Square, accum_out=ssl[:, 0:1]),
                         R=[x2l], W=[junk4, ssl])
                    rstd_from_ss(ssl, rsl, 1, 1.0 / D)
                    S.op("dve", lambda h: h.scalar_tensor_tensor(out=x2l[:], in0=x2l[:], scalar=rsl[:, 0:1], in1=gfin[:],
                                                                 op0=ALU.mult, op1=ALU.mult), R=[x2l, rsl, gfin], W=[x2l])
                    S.dma(y_out[rows, :], x2l[:], R=[x2l], W=[y_buf], owner=x2l)
          S.barrier()

        S.barrier()
    return nc


def _prep_inputs(x, c, ctx, c_ctx, w_ada, b_ada, g_mix, w_in, conv_w, m_gate_b, m_norm_g, g_gate_w, g_gate_b,
                 g_norm_g, w_out, g_ffn, w_gu, w_down, g_final):
    f = np.float32

    def col(v, n):
        return np.ascontiguousarray(np.asarray(v, f).reshape(n, 128).T)

    w_in0 = np.asarray(w_in[0], f)
    sizes = (1024, 512, 512, 16, 256, 256, 512, 512, 32)
    offs = np.cumsum((0,) + sizes)
    mqk, mv, mo, mg, gq, gk, gv, gr, glr = [w_in0[:, offs[i]:offs[i + 1]] for i in range(9)]
    shared = dict(
        w_ada=np.ascontiguousarray(np.asarray(w_ada[0], f)),
        b_ada_c=col(b_ada[0], 48), gmix_c=col(g_mix[0], 8), gffn_c=col(g_ffn[0], 8),
        gfin_r=np.ascontiguousarray(np.asarray(g_final, f).reshape(1, D)),
        gn_c=col(np.concatenate([np.asarray(m_norm_g[0], f), np.asarray(g_norm_g[0], f)]), 8),
        w_out=np.ascontiguousarray(np.asarray(w_out[0], f)),
        w_gu=np.ascontiguousarray(np.asarray(w_gu[0], f)),
        w_down=np.ascontiguousarray(np.asarray(w_down[0], f)),
    )
    per_mirror = []
    for mir in range(2):
        if mir == 0:
            mg_l, glr_l = mg, glr
            mgb = np.asarray(m_gate_b[0], f)
            ggw = np.asarray(g_gate_w[0], f)
            ggb = np.asarray(g_gate_b[0], f)
            cw = np.asarray(conv_w[0], f)
        else:
            mg_l = np.concatenate([mg[:, 8:16], mg[:, 0:8]], 1)
            glr_l = np.concatenate([glr[:, 16:32], glr[:, 0:16]], 1)
            mgb = np.concatenate([np.asarray(m_gate_b[0], f)[8:16], np.asarray(m_gate_b[0], f)[0:8]])
            ggw = np.asarray(g_gate_w[0], f)[::-1]
            ggb = np.asarray(g_gate_b[0], f)[::-1]
            cw = np.asarray(conv_w[0], f)[::-1, ::-1]
        w_fm = np.ascontiguousarray(np.concatenate([mqk, gq, gk, glr_l], 1))
        w_tm = np.ascontiguousarray(np.concatenate([mv, mo, gv, gr, mg_l], 1))
        convw_c = np.ascontiguousarray(cw.reshape(9, 8, 128).transpose(2, 0, 1))
        ggw_e = np.ascontiguousarray(np.concatenate([ggw, ggb[:, None, :]], 1))
        per_mirror.append(dict(w_fm=w_fm, w_tm=w_tm, convw_c=convw_c, mgb_r=np.ascontiguousarray(mgb.reshape(1, 16)),
                               ggw_e=ggw_e))
    in_maps = []
    for core in range(8):
        b, mir = core // 2, core % 2
        xl = np.asarray(x[b], f)
        cl = np.asarray(ctx[b], f)
        if mir:
            xl = xl[::-1]
            cl = cl[::-1]
        cvec = np.stack([col(c[b], 8), col(c_ctx, 8)], -1)
        m = dict(x_l=np.ascontiguousarray(xl), ctx_l=np.ascontiguousarray(cl), cvec=np.ascontiguousarray(cvec))
        m.update(shared)
        m.update(per_mirror[mir])
        in_maps.append(m)
    return in_maps


_NC_CACHE = {}


def kernel(**inputs):
    in_maps = _prep_inputs(**inputs)
    if "nc" not in _NC_CACHE:
        _NC_CACHE["nc"] = build_program()
    nc = _NC_CACHE["nc"]
    res = run_bass_kernel_spmd(nc, in_maps, core_ids=list(range(8)))
    out = np.zeros((4, SEQ, D), np.float32)
    for core in range(8):
        b, mir = core // 2, core % 2
        y = np.asarray(res.results[core]["y"])
        if mir:
            out[b, OWN:] = y[::-1]
        else:
            out[b, :OWN] = y
    if DEBUG:
        kernel.last = res
    return out
```
